# Optimizing a Trainium2 kernel written in Bass

```python
import jax, jax.numpy as jnp
from jax import lax
import numpy as np

D_MODEL = 1024
BATCH = 2
SEQ = 16384
DEPTH = 4

GRID_W = 64
CTX_LEN = 256

N_GROUPS = 4
GROUP_W = D_MODEL // N_GROUPS
HEAD_DIM = GROUP_W // 4

SGU_HEADS = 4
SGU_CHUNK = 128
MLSTM_HEADS = 4
MLSTM_CHUNK = 128
QK_CONV = 3
ATT_HEADS = 4
ATT_KV_HEADS = 2
ATT_GROUP = ATT_HEADS // ATT_KV_HEADS
ATT_BLOCK = 128
WINDOW = 128
ROPE_BASE = 10000.0
AXIS_ROT = HEAD_DIM // 2
AXIS_FREQ = AXIS_ROT // 2
POOL_WINDOWS = (2, 4, 8, 16)
POOL_CH = GROUP_W // len(POOL_WINDOWS)
FFN_HIDDEN = -(-(8 * D_MODEL) // (3 * 256)) * 256

EPS = 1e-6

SGU_COLS = 2 * GROUP_W
MLSTM_COLS = 4 * GROUP_W + 4 * MLSTM_HEADS
ATT_Q = ATT_HEADS * HEAD_DIM
ATT_KV = ATT_KV_HEADS * HEAD_DIM
ATT_COLS = ATT_Q + 2 * ATT_KV
POOL_COLS = GROUP_W
OFF_MLSTM = SGU_COLS
OFF_ATT = OFF_MLSTM + MLSTM_COLS
OFF_POOL = OFF_ATT + ATT_COLS
IN_COLS = OFF_POOL + POOL_COLS

kernel_name = 'hybrid_parallel_groups_flow_backbone'

F32 = jnp.float32


def rms_norm(x, g):
    xf = x.astype(F32)
    y = xf * lax.rsqrt(jnp.mean(xf * xf, axis=-1, keepdims=True) + EPS)
    return (y * g.astype(F32)).astype(x.dtype)


def layer_norm(x):
    xf = x.astype(F32)
    mu = jnp.mean(xf, axis=-1, keepdims=True)
    xc = xf - mu
    return xc * lax.rsqrt(jnp.mean(xc * xc, axis=-1, keepdims=True) + EPS)


def modulate(h, shift, scale):
    return h * (1 + scale) + shift


def axial_rope_tables(n_tokens):
    rows = n_tokens // GRID_W
    row = jnp.broadcast_to(jnp.arange(rows)[:, None], (rows, GRID_W)).reshape(-1).astype(F32)
    col = jnp.broadcast_to(jnp.arange(GRID_W)[None, :], (rows, GRID_W)).reshape(-1).astype(F32)
    inv = jnp.power(ROPE_BASE, -jnp.arange(AXIS_FREQ, dtype=F32) * 2.0 / AXIS_ROT)
    ar = row[:, None] * inv
    ac = col[:, None] * inv
    ang = jnp.concatenate([ar, ar, ac, ac], axis=-1)
    return jnp.cos(ang), jnp.sin(ang)


def apply_rope(x, cos, sin):
    xs = x.reshape(x.shape[:-1] + (2, 2, AXIS_FREQ))
    rot = jnp.stack([-xs[..., 1, :], xs[..., 0, :]], axis=-2).reshape(x.shape)
    c = cos[None, :, None, :].astype(x.dtype)
    s = sin[None, :, None, :].astype(x.dtype)
    return x * c + rot * s


def spatial_gating(z, w_s, b_s):
    B, N, _ = z.shape
    z = jax.nn.gelu(z)
    u, v = z[..., :GROUP_W], z[..., GROUP_W:]
    nc = N // SGU_CHUNK
    vh = layer_norm(v.reshape(B, nc, SGU_CHUNK, SGU_HEADS, GROUP_W // SGU_HEADS)).astype(z.dtype)
    mixed = jnp.einsum('hts,bnshd->bnthd', w_s, vh) + b_s.T[None, None, :, :, None]
    return u * mixed.reshape(B, N, GROUP_W)


def short_conv(x, w):
    n = x.shape[1]
    p = QK_CONV // 2
    xp = jnp.pad(x, ((0, 0), (p, p), (0, 0)))
    out = xp[:, 0:n] * w[0]
    for j in range(1, QK_CONV):
        out = out + xp[:, j:j + n] * w[j]
    return out


def mlstm_chunk_scan(q, k, v, log_i, log_f, state):
    B, H, N, dh = q.shape
    nc = N // MLSTM_CHUNK

    def chunks(a):
        a = a.reshape((B, H, nc, MLSTM_CHUNK) + a.shape[3:])
        return jnp.moveaxis(a, 2, 0)

    tril = jnp.tril(jnp.ones((MLSTM_CHUNK, MLSTM_CHUNK), dtype=bool))

    def step(carry, xs):
        C, n, m = carry
        qc, kc, vc, li, lf = xs
        b = jnp.cumsum(lf, axis=-1)
        d_ts = jnp.where(tril, b[..., :, None] - b[..., None, :] + li[..., None, :], -jnp.inf)
        m_t = jnp.maximum(b + m[..., None], jnp.max(d_ts, axis=-1))
        inter = jnp.exp(b + m[..., None] - m_t)
        s = jnp.einsum('bhtd,bhsd->bhts', qc, kc) * jnp.exp(d_ts - m_t[..., None])
        num = inter[..., None] * jnp.einsum('bhtd,bhde->bhte', qc, C) + jnp.einsum('bhts,bhse->bhte', s, vc)
        den = inter * jnp.einsum('bhtd,bhd->bht', qc, n) + jnp.sum(s, axis=-1)
        h = num / jnp.maximum(jnp.abs(den), jnp.exp(-m_t))[..., None]
        b_end = b[..., -1]
        g = b_end[..., None] - b + li
        m_new = jnp.maximum(b_end + m, jnp.max(g, axis=-1))
        decay = jnp.exp(b_end + m - m_new)
        w = jnp.exp(g - m_new[..., None])
        C_new = decay[..., None, None] * C + jnp.einsum('bhs,bhsd,bhse->bhde', w, kc, vc)
        n_new = decay[..., None] * n + jnp.einsum('bhs,bhsd->bhd', w, kc)
        return (C_new, n_new, m_new), h

    state, hs = lax.scan(step, state, (chunks(q), chunks(k), chunks(v), chunks(log_i), chunks(log_f)))
    h = jnp.moveaxis(hs, 0, 2).reshape(B, H, N, dh)
    return h, state


def mlstm_prep(z, conv_w, gate_b):
    B, N, _ = z.shape
    G = GROUP_W
    qk = jax.nn.silu(short_conv(z[..., :2 * G], conv_w))

    def heads(a):
        return a.reshape(B, N, MLSTM_HEADS, -1).transpose(0, 2, 1, 3).astype(F32)

    q = heads(qk[..., :G])
    k = heads(qk[..., G:]) * (HEAD_DIM ** -0.5)
    v = heads(z[..., 2 * G:3 * G])
    o = jax.nn.sigmoid(z[..., 3 * G:4 * G])
    gates = (z[..., 4 * G:].astype(F32).reshape(B, N, 4, MLSTM_HEADS) + gate_b.astype(F32)).transpose(2, 0, 3, 1)
    gate_logs = (gates[0], jax.nn.log_sigmoid(gates[1]), gates[2], jax.nn.log_sigmoid(gates[3]))
    return q, k, v, o, gate_logs


def mlstm_directions(q, k, v, gate_logs, state_f, state_b):
    li_f, lf_f, li_b, lf_b = gate_logs
    h_f, st_f = mlstm_chunk_scan(q, k, v, li_f, lf_f, state_f)
    flip = lambda a: jnp.flip(a, axis=2)
    h_b, st_b = mlstm_chunk_scan(flip(q), flip(k), flip(v), flip(li_b), flip(lf_b), state_b)
    return h_f + flip(h_b), st_f, st_b


def mlstm_out(h, o, norm_g):
    B, H, N, dh = h.shape
    hn = layer_norm(h.transpose(0, 2, 1, 3)) * norm_g.astype(F32).reshape(H, dh)
    return hn.reshape(B, N, H * dh).astype(o.dtype) * o


def mlstm_mixer(zx, zc, conv_w, gate_b, norm_g, update_ctx):
    B = zx.shape[0]
    zero = (jnp.zeros((B, MLSTM_HEADS, HEAD_DIM, HEAD_DIM), F32),
            jnp.zeros((B, MLSTM_HEADS, HEAD_DIM), F32),
            jnp.zeros((B, MLSTM_HEADS), F32))
    qc, kc, vc, oc, gc = mlstm_prep(zc, conv_w, gate_b)
    h_c, st_f, st_b = mlstm_directions(qc, kc, vc, gc, zero, zero)
    qx, kx, vx, ox, gx = mlstm_prep(zx, conv_w, gate_b)
    h_x, _, _ = mlstm_directions(qx, kx, vx, gx, st_f, st_b)
    out_x = mlstm_out(h_x, ox, norm_g)
    out_c = mlstm_out(h_c, oc, norm_g) if update_ctx else None
    return out_x, out_c


def band_attention(q, k, v, k_ctx, v_ctx, sink):
    B, N = q.shape[:2]
    L = ATT_BLOCK
    nb = N // L
    scale = HEAD_DIM ** -0.5
    qb = q.reshape(B, nb, L, ATT_KV_HEADS, ATT_GROUP, HEAD_DIM)
    pad = ((0, 0), (L, L), (0, 0), (0, 0))
    kp = jnp.pad(k, pad).reshape(B, nb + 2, L, ATT_KV_HEADS, HEAD_DIM)
    vp = jnp.pad(v, pad).reshape(B, nb + 2, L, ATT_KV_HEADS, HEAD_DIM)
    band = lambda a: jnp.concatenate([a[:, :-2], a[:, 1:-1], a[:, 2:]], axis=2)
    kb, vb = band(kp), band(vp)
    s_loc = jnp.einsum('bnqhgd,bnshd->bnhgqs', qb, kb).astype(F32) * scale
    s_ctx = jnp.einsum('bnqhgd,bchd->bnhgqc', qb, k_ctx).astype(F32) * scale
    qpos = jnp.arange(nb)[:, None] * L + jnp.arange(L)[None, :]
    kpos = (jnp.arange(nb)[:, None] - 1) * L + jnp.arange(3 * L)[None, :]
    valid = (jnp.abs(qpos[:, :, None] - kpos[:, None, :]) <= WINDOW) & ((kpos >= 0) & (kpos < N))[:, None, :]
    s_loc = jnp.where(valid[None, :, None, None], s_loc, -jnp.inf)
    s_sink = jnp.broadcast_to(sink.astype(F32).reshape(ATT_KV_HEADS, ATT_GROUP)[None, None, :, :, None, None],
                              s_loc.shape[:-1] + (1,))
    p = jax.nn.softmax(jnp.concatenate([s_loc, s_ctx, s_sink], axis=-1), axis=-1).astype(v.dtype)
    n_loc = 3 * L
    n_ctx = k_ctx.shape[1]
    o = (jnp.einsum('bnhgqs,bnshd->bnqhgd', p[..., :n_loc], vb)
         + jnp.einsum('bnhgqc,bchd->bnqhgd', p[..., n_loc:n_loc + n_ctx], v_ctx))
    return o.reshape(B, N, ATT_HEADS * HEAD_DIM)


def context_attention(q, k, v, sink):
    B, C = q.shape[:2]
    qg = q.reshape(B, C, ATT_KV_HEADS, ATT_GROUP, HEAD_DIM)
    s = jnp.einsum('bqhgd,bchd->bhgqc', qg, k).astype(F32) * (HEAD_DIM ** -0.5)
    s_sink = jnp.broadcast_to(sink.astype(F32).reshape(ATT_KV_HEADS, ATT_GROUP)[None, :, :, None, None],
                              s.shape[:-1] + (1,))
    p = jax.nn.softmax(jnp.concatenate([s, s_sink], axis=-1), axis=-1)[..., :-1].astype(v.dtype)
    return jnp.einsum('bhgqc,bchd->bqhgd', p, v).reshape(B, C, ATT_HEADS * HEAD_DIM)


def attn_mixer(zx, zc, cos, sin, sink, update_ctx):
    def split(z):
        B, N, _ = z.shape
        q = z[..., :ATT_Q].reshape(B, N, ATT_HEADS, HEAD_DIM)
        k = z[..., ATT_Q:ATT_Q + ATT_KV].reshape(B, N, ATT_KV_HEADS, HEAD_DIM)
        v = z[..., ATT_Q + ATT_KV:].reshape(B, N, ATT_KV_HEADS, HEAD_DIM)
        return q, k, v
    qx, kx, vx = split(zx)
    qc, kc, vc = split(zc)
    out_x = band_attention(apply_rope(qx, cos, sin), apply_rope(kx, cos, sin), vx, kc, vc, sink)
    out_c = context_attention(qc, kc, vc, sink) if update_ctx else None
    return out_x, out_c


def pool_mixer(z, pool_w, pool_scale):
    B, N, _ = z.shape
    zg = z.astype(F32).reshape(B, N, len(POOL_WINDOWS), POOL_CH)
    csum = jnp.concatenate([jnp.zeros((B, 1) + zg.shape[2:], F32), jnp.cumsum(zg, axis=1)], axis=1)
    t = jnp.arange(N)
    means = []
    for gi, w in enumerate(POOL_WINDOWS):
        lo = jnp.clip(t - w // 2, 0, N)
        hi = jnp.clip(t + w - w // 2, 0, N)
        cnt = (hi - lo).astype(F32)
        means.append((csum[:, hi, gi] - csum[:, lo, gi]) / cnt[None, :, None])
    pooled = jnp.stack(means, axis=2)
    y = jnp.einsum('bngc,gcd->bngd', (pooled - zg).astype(z.dtype), pool_w)
    return y.reshape(B, N, GROUP_W) * pool_scale


def token_mixers(zx, zc, cos, sin, sgu_w, sgu_b, conv_w, gate_b, mnorm_g, sink, pool_w, pool_scale, update_ctx):
    seg = lambda z, a, b: z[..., a:b]
    a_x = spatial_gating(seg(zx, 0, OFF_MLSTM), sgu_w, sgu_b)
    b_x, b_c = mlstm_mixer(seg(zx, OFF_MLSTM, OFF_ATT), seg(zc, OFF_MLSTM, OFF_ATT), conv_w, gate_b, mnorm_g, update_ctx)
    c_x, c_c = attn_mixer(seg(zx, OFF_ATT, OFF_POOL), seg(zc, OFF_ATT, OFF_POOL), cos, sin, sink, update_ctx)
    d_x = pool_mixer(seg(zx, OFF_POOL, IN_COLS), pool_w, pool_scale)
    out_x = jnp.concatenate([a_x, b_x, c_x, d_x], axis=-1)
    out_c = None
    if update_ctx:
        a_c = spatial_gating(seg(zc, 0, OFF_MLSTM), sgu_w, sgu_b)
        d_c = pool_mixer(seg(zc, OFF_POOL, IN_COLS), pool_w, pool_scale)
        out_c = jnp.concatenate([a_c, b_c, c_c, d_c], axis=-1)
    return out_x, out_c


def swiglu(h, w_in_, w_out_):
    gate, up = jnp.split(h @ w_in_, 2, axis=-1)
    return (jax.nn.silu(gate) * up) @ w_out_


def trunk_layer(x, ctx, c, c_ctx, cos, sin, w_mod, b_mod, norm_g, w_in, w_out, sgu_w, sgu_b, conv_w, gate_b,
                mnorm_g, sink, pool_w, pool_scale, w_ffn_in, w_ffn_out, update_ctx):
    sh_a, sc_a, g_a, sh_f, sc_f, g_f = jnp.split((jax.nn.silu(c) @ w_mod + b_mod)[:, None, :], 6, axis=-1)
    csh_a, csc_a, cg_a, csh_f, csc_f, cg_f = jnp.split((jax.nn.silu(c_ctx) @ w_mod + b_mod)[None, None, :], 6, axis=-1)
    zx = modulate(rms_norm(x, norm_g[0]), sh_a, sc_a) @ w_in
    zc = modulate(rms_norm(ctx, norm_g[0]), csh_a, csc_a) @ w_in
    mix_x, mix_c = token_mixers(zx, zc, cos, sin, sgu_w, sgu_b, conv_w, gate_b, mnorm_g, sink, pool_w, pool_scale,
                                update_ctx)
    x = x + g_a * rms_norm(mix_x @ w_out, norm_g[1])
    x = x + g_f * rms_norm(swiglu(modulate(rms_norm(x, norm_g[2]), sh_f, sc_f), w_ffn_in, w_ffn_out), norm_g[3])
    if update_ctx:
        ctx = ctx + cg_a * rms_norm(mix_c @ w_out, norm_g[1])
        ctx = ctx + cg_f * rms_norm(swiglu(modulate(rms_norm(ctx, norm_g[2]), csh_f, csc_f), w_ffn_in, w_ffn_out),
                                    norm_g[3])
    return x, ctx


def setup_inputs(seed: int = 0) -> dict:
    key = jax.random.key(seed)
    ks = jax.random.split(key, 19)
    nrm = lambda k, shape, s: jax.random.normal(k, shape, F32) * s
    x = nrm(ks[0], (BATCH, SEQ, D_MODEL), 1.0)
    c = nrm(ks[1], (BATCH, D_MODEL), 1.0)
    ctx = nrm(ks[2], (BATCH, CTX_LEN, D_MODEL), 1.0)
    c_ctx = nrm(ks[3], (D_MODEL,), 1.0)
    w_mod = nrm(ks[4], (DEPTH, D_MODEL, 6 * D_MODEL), 0.5 * D_MODEL ** -0.5)
    b_mod = nrm(ks[5], (DEPTH, 6 * D_MODEL), 0.02)
    norm_g = 1.0 + nrm(ks[6], (DEPTH, 4, D_MODEL), 0.05)
    w_in = nrm(ks[7], (DEPTH, D_MODEL, IN_COLS), D_MODEL ** -0.5)
    w_out = nrm(ks[8], (DEPTH, D_MODEL, D_MODEL), D_MODEL ** -0.5)
    sgu_w = nrm(ks[9], (DEPTH, SGU_HEADS, SGU_CHUNK, SGU_CHUNK), SGU_CHUNK ** -0.5)
    sgu_b = 1.0 + nrm(ks[10], (DEPTH, SGU_HEADS, SGU_CHUNK), 0.05)
    mlstm_conv_w = nrm(ks[11], (DEPTH, QK_CONV, 2 * GROUP_W), QK_CONV ** -0.5)
    f_bias = jnp.linspace(3.0, 6.0, MLSTM_HEADS)
    is_forget = jnp.array([0.0, 1.0, 0.0, 1.0], dtype=F32)
    mlstm_gate_b = nrm(ks[12], (DEPTH, 4, MLSTM_HEADS), 0.1) + is_forget[None, :, None] * f_bias[None, None, :]
    mlstm_norm_g = 1.0 + nrm(ks[13], (DEPTH, GROUP_W), 0.05)
    attn_sink = nrm(ks[14], (DEPTH, ATT_HEADS), 1.0)
    pool_w = nrm(ks[15], (DEPTH, len(POOL_WINDOWS), POOL_CH, POOL_CH), POOL_CH ** -0.5)
    pool_scale = 1.0 + nrm(ks[16], (DEPTH, GROUP_W), 0.1)
    w_ffn_in = nrm(ks[17], (DEPTH, D_MODEL, 2 * FFN_HIDDEN), D_MODEL ** -0.5)
    w_ffn_out = nrm(ks[18], (DEPTH, FFN_HIDDEN, D_MODEL), FFN_HIDDEN ** -0.5)
    return {'x': x, 'c': c, 'ctx': ctx, 'c_ctx': c_ctx, 'w_mod': w_mod, 'b_mod': b_mod, 'norm_g': norm_g,
            'w_in': w_in, 'w_out': w_out, 'sgu_w': sgu_w, 'sgu_b': sgu_b, 'mlstm_conv_w': mlstm_conv_w,
            'mlstm_gate_b': mlstm_gate_b, 'mlstm_norm_g': mlstm_norm_g, 'attn_sink': attn_sink,
            'pool_w': pool_w, 'pool_scale': pool_scale, 'w_ffn_in': w_ffn_in, 'w_ffn_out': w_ffn_out}


def reference(x, c, ctx, c_ctx, w_mod, b_mod, norm_g, w_in, w_out, sgu_w, sgu_b, mlstm_conv_w, mlstm_gate_b,
              mlstm_norm_g, attn_sink, pool_w, pool_scale, w_ffn_in, w_ffn_out):
    cos, sin = axial_rope_tables(x.shape[1])
    for l in range(DEPTH):
        x, ctx = trunk_layer(x, ctx, c, c_ctx, cos, sin, w_mod[l], b_mod[l], norm_g[l], w_in[l], w_out[l],
                             sgu_w[l], sgu_b[l], mlstm_conv_w[l], mlstm_gate_b[l], mlstm_norm_g[l], attn_sink[l],
                             pool_w[l], pool_scale[l], w_ffn_in[l], w_ffn_out[l], l < DEPTH - 1)
    return x
```

```python
import math
from contextlib import ExitStack

import numpy as np
import concourse.bass as bass
import concourse.mybir as mybir
from concourse.bass_utils import run_bass_kernel_spmd

F32 = mybir.dt.float32
BF16 = mybir.dt.bfloat16
I32 = mybir.dt.int32
AF = mybir.ActivationFunctionType
ALU = mybir.AluOpType

D = 1024
KC = 8
DEPTH = 4
CTX = 256
GRID_W = 64
EPS = 1e-6
FFN_H = 2816
HT = FFN_H // 128
NEG = -30000.0
LN8 = math.log(0.125)
MAXOPS = None


class Trk:
    __slots__ = ("name", "w", "r")

    def __init__(self, name):
        self.name = name
        self.w = None
        self.r = []


class Op:
    __slots__ = ("eng", "fn", "deps", "dma", "sem", "val", "sig", "inc", "tiled")

    def __init__(self, eng, fn, deps, dma):
        self.eng = eng
        self.fn = fn
        self.deps = deps
        self.dma = dma
        self.sem = None
        self.val = 0
        self.sig = False
        self.inc = None
        self.tiled = False


ENGS = ("pe", "act", "dve", "pool", "sp")


class Prog:
    def __init__(self):
        self.ops = []

    def add(self, eng, fn, reads=(), writes=(), dma=False, inc=None, tiled=False):
        i = len(self.ops)
        deps = set()
        for t in reads:
            if t.w is not None:
                deps.add(t.w)
        for t in writes:
            if t.w is not None:
                deps.add(t.w)
            deps.update(t.r)
        deps.discard(i)
        for t in reads:
            t.r.append(i)
        for t in writes:
            t.w = i
            t.r = []
        op = Op(eng, fn, deps, dma)
        op.inc = inc
        op.tiled = tiled
        self.ops.append(op)
        return i

    def emit(self, nc, es, n_dma_sems=40):
        ops = self.ops
        for op in ops:
            if op.dma:
                op.sig = True
        for op in ops:
            for d in op.deps:
                dop = ops[d]
                if dop.eng == "pe" and op.eng == "pe" and not dop.dma and not op.dma:
                    continue
                dop.sig = True
        esem = {e: es.enter_context(nc.semaphore("s_" + e)) for e in ENGS}
        dsems = [es.enter_context(nc.semaphore("d%d" % k)) for k in range(n_dma_sems)]
        dcount = [0] * n_dma_sems
        dprev = [None] * n_dma_sems
        ecount = {e: 0 for e in ENGS}
        extra = {}
        n_sp = n_dma_sems - 14
        pools = {"sp": list(range(0, n_sp)), "pool": list(range(n_sp, n_dma_sems - 2)), "cc": list(range(n_dma_sems - 2, n_dma_sems))}
        cnt = {"sp": 0, "pool": 0, "cc": 0}
        for i, op in enumerate(ops):
            if op.dma:
                kind = "cc" if op.inc == 1 else ("pool" if op.eng == "pool" else "sp")
                lst = pools[kind]
                k = lst[cnt[kind] % len(lst)]
                cnt[kind] += 1
                op.sem = dsems[k]
                if op.inc is None:
                    op.inc = 16
                dcount[k] += op.inc
                op.val = dcount[k]
                if dprev[k] is not None:
                    extra[i] = dprev[k]
                dprev[k] = i
            elif op.sig:
                ecount[op.eng] += 1
                op.sem = esem[op.eng]
                op.inc = 1
                op.val = ecount[op.eng]
        streams = {e: [] for e in ENGS}
        for i, op in enumerate(ops):
            streams[op.eng].append(i)
        final_waits = [(dsems[k], dcount[k]) for k in range(n_dma_sems) if dcount[k] > 0]

        def run(engname, eng):
            seen = {}
            pe_mode = False
            for i in streams[engname]:
                op = ops[i]
                if engname == "pe" and not op.dma and op.tiled != pe_mode:
                    eng.drain()
                    pe_mode = op.tiled
                deps = set(op.deps)
                if i in extra:
                    deps.add(extra[i])
                need = {}
                for d in deps:
                    dop = ops[d]
                    if not dop.sig:
                        continue
                    if dop.eng == "pe" and engname == "pe" and not dop.dma and not op.dma:
                        continue
                    key = id(dop.sem)
                    if seen.get(key, 0) >= dop.val:
                        continue
                    if key not in need or need[key][1] < dop.val:
                        need[key] = (dop.sem, dop.val)
                for key, (sem, val) in need.items():
                    eng.wait_ge(sem, val)
                    seen[key] = val
                ins = op.fn(eng)
                if op.sig:
                    ins.then_inc(op.sem, op.inc)
            if engname == "sp":
                for sem, val in final_waits:
                    eng.wait_ge(sem, val)

        with nc.Block() as block:
            block.tensor(lambda e: run("pe", e))
            block.scalar(lambda e: run("act", e))
            block.vector(lambda e: run("dve", e))
            block.gpsimd(lambda e: run("pool", e))
            block.sync(lambda e: run("sp", e))
        return len(ops)


G = 256
OFF_M = 512
OFF_A = OFF_M + 4 * G + 16
OFF_P = OFF_A + 512
ROPE_PERM = None


def _rope_perm():
    perm = np.zeros(64, np.int64)
    sgn = np.zeros(64, np.float32)
    for d in range(64):
        g32, o = divmod(d, 32)
        if o < 16:
            perm[d] = g32 * 32 + o + 16
            sgn[d] = -1.0
        else:
            perm[d] = g32 * 32 + o - 16
            sgn[d] = 1.0
    return perm, sgn


def _rope_tables(pos):
    pos = np.asarray(pos)
    row = (pos // GRID_W).astype(np.float32)
    col = (pos % GRID_W).astype(np.float32)
    inv = np.power(np.float32(10000.0), -np.arange(16, dtype=np.float32) * np.float32(2.0) / np.float32(32.0)).astype(np.float32)
    ar = row[:, None] * inv
    ac = col[:, None] * inv
    ang = np.concatenate([ar, ar, ac, ac], axis=-1).astype(np.float32)
    _, sgn = _rope_perm()
    return np.cos(ang).T.astype(np.float32), (np.sin(ang) * sgn[None, :]).T.astype(np.float32)


def _pool_tables():
    wins = (2, 4, 8, 16)

    def build(first, last, n_total_tiles_hint):
        M = np.zeros((3, 4, 128, 128), np.float32)
        IC = np.zeros((4, 128), np.float32)
        for g, w in enumerate(wins):
            for t in range(128):
                lo = t - w // 2
                hi = t + w - w // 2
                if first:
                    lo = max(lo, 0)
                if last:
                    hi = min(hi, 128)
                cnt = hi - lo
                IC[g, t] = 1.0 / cnt
                for s in range(lo, hi):
                    if s < 0:
                        M[0, g, s + 128, t] += 1.0
                    elif s >= 128:
                        M[2, g, s - 128, t] += 1.0
                    else:
                        M[1, g, s, t] += 1.0
                M[1, g, t, t] -= cnt
        return M, IC

    out = [build(True, False, 0), build(False, False, 0), build(False, True, 0), build(True, False, 0), build(False, True, 0)]
    return out


def _host_prep(inputs, NT):
    x = np.asarray(inputs["x"], np.float32)
    B, N, _ = x.shape
    assert N == 4 * NT * 128
    NTOK = NT * 128
    c = np.asarray(inputs["c"], np.float32)
    ctx = np.asarray(inputs["ctx"], np.float32)
    c_ctx = np.asarray(inputs["c_ctx"], np.float32)
    w_in = np.asarray(inputs["w_in"], np.float32)
    perm, _ = _rope_perm()
    L = w_in.shape[0]

    def fmcols():
        cols = []
        cols += list(range(0, 256))
        cols += list(range(OFF_M, OFF_M + 256))
        cols += list(range(OFF_M + 256, OFF_M + 512))
        aq = [OFF_A + h * 64 + d for h in range(4) for d in range(64)]
        aqp = [OFF_A + h * 64 + perm[d] for h in range(4) for d in range(64)]
        ak = [OFF_A + 256 + g * 64 + d for g in range(2) for _dup in range(2) for d in range(64)]
        akp = [OFF_A + 256 + g * 64 + perm[d] for g in range(2) for _dup in range(2) for d in range(64)]
        cols += aq + aqp + ak + akp
        return np.array(cols)

    def tmcols():
        cols = []
        cols += list(range(256, 512))
        cols += list(range(OFF_P, OFF_P + 256))
        cols += list(range(OFF_M + 768, OFF_M + 1024))
        cols += list(range(OFF_M + 512, OFF_M + 768))
        cols += list(range(OFF_A + 384, OFF_A + 512))
        gbase = OFF_M + 1024
        for gt in (0, 2, 1, 3):
            cols += [gbase + gt * 4 + h for h in range(4)]
        return np.array(cols)

    w_in_fm = np.ascontiguousarray(w_in[:, :, fmcols()])
    w_in_tm = np.zeros((L, D, 1280), np.float32)
    w_in_tm[:, :, :1168] = w_in[:, :, tmcols()]
    b_mod = np.asarray(inputs["b_mod"], np.float32)
    b_modT = np.ascontiguousarray(b_mod.reshape(L, 48, 128).transpose(2, 0, 1))
    norm_g = np.asarray(inputs["norm_g"], np.float32)
    norm_gT = np.ascontiguousarray(norm_g.reshape(L, 4, 8, 128).transpose(3, 0, 1, 2))
    sgu_w = np.asarray(inputs["sgu_w"], np.float32)
    sgu_wT = np.ascontiguousarray(sgu_w.transpose(3, 0, 1, 2))
    sgu_b = np.asarray(inputs["sgu_b"], np.float32)
    sgu_brep = np.zeros((128, L, 2, 128), np.float32)
    for ft in range(2):
        for hh in range(2):
            sgu_brep[hh * 64:(hh + 1) * 64, :, ft, :] = sgu_b[None, :, 2 * ft + hh, :]
    conv_w = np.asarray(inputs["mlstm_conv_w"], np.float32)
    conv_wT = np.ascontiguousarray(conv_w.reshape(L, 3, 4, 128).transpose(3, 0, 2, 1))
    gate_b = np.asarray(inputs["mlstm_gate_b"], np.float32)
    gb = np.concatenate([gate_b[:, 0], gate_b[:, 2], gate_b[:, 1], gate_b[:, 3]], axis=-1)
    gate_brep = np.ascontiguousarray(np.broadcast_to(gb[None], (128, L, 16)))
    mng = np.asarray(inputs["mlstm_norm_g"], np.float32)
    mnorm_grep = np.ascontiguousarray(np.broadcast_to(mng[None], (128, L, 256)))
    sink = np.asarray(inputs["attn_sink"], np.float32)
    sink_rep = np.ascontiguousarray(np.broadcast_to(sink[None], (128, L, 4)))
    pool_w = np.asarray(inputs["pool_w"], np.float32)
    pool_w2 = np.zeros((128, L, 2, 64), np.float32)
    for pr in range(2):
        for hh in range(2):
            pool_w2[hh * 64:(hh + 1) * 64, :, pr, :] = pool_w[:, 2 * pr + hh].transpose(1, 0, 2)
    pool_scale = np.asarray(inputs["pool_scale"], np.float32)
    pool_scaleT = np.ascontiguousarray(pool_scale.reshape(L, 2, 128).transpose(2, 0, 1))

    ptab = _pool_tables()
    pool_M = np.zeros((128, 5, 3, 4, 128), np.float32)
    pool_IC = np.zeros((128, 5, 2, 128), np.float32)
    for v in range(5):
        M, IC = ptab[v]
        pool_M[:, v] = M.transpose(2, 0, 1, 3)
        for pr in range(2):
            for hh in range(2):
                pool_IC[hh * 64:(hh + 1) * 64, v, pr, :] = IC[2 * pr + hh][None, :]
    tt = np.arange(128)
    band_prev = np.where(tt[None, :] >= tt[:, None], 0.0, NEG).astype(np.float32)
    band_next = np.where(tt[None, :] <= tt[:, None], 0.0, NEG).astype(np.float32)
    tri_f = (tt[:, None] <= tt[None, :]).astype(np.float32)
    tri_b = (tt[:, None] >= tt[None, :]).astype(np.float32)

    shared = dict(
        w_mod=np.asarray(inputs["w_mod"], np.float32), b_modT=b_modT, norm_gT=norm_gT,
        w_in_fm=w_in_fm, w_in_tm=w_in_tm, w_out=np.asarray(inputs["w_out"], np.float32),
        sgu_wT=sgu_wT, sgu_brep=sgu_brep, conv_wT=conv_wT, gate_brep=gate_brep, mnorm_grep=mnorm_grep,
        sink_rep=sink_rep, pool_w2=pool_w2, pool_scaleT=pool_scaleT,
        w_ffn_in=np.asarray(inputs["w_ffn_in"], np.float32), w_ffn_out=np.asarray(inputs["w_ffn_out"], np.float32),
        tri_f=tri_f, tri_b=tri_b, ident=np.eye(128, dtype=np.float32),
    )
    maps = []
    for r in range(8):
        b, q = divmod(r, 4)
        m = dict(shared)
        m["x"] = np.ascontiguousarray(x[b, q * NTOK:(q + 1) * NTOK])
        m["ctx"] = np.ascontiguousarray(ctx[b])
        cv = np.stack([c[b], c_ctx], axis=-1)
        m["cvec"] = np.ascontiguousarray(cv.reshape(8, 128, 2).transpose(1, 0, 2))
        pos = np.concatenate([np.arange(q * NTOK - 128, q * NTOK), np.arange(q * NTOK, (q + 1) * NTOK),
                              np.arange((q + 1) * NTOK, (q + 1) * NTOK + 128)])
        pos = np.clip(pos, 0, N - 1)
        cs, sn = _rope_tables(pos)
        cs = np.concatenate([cs, np.ones((64, 256), np.float32)], axis=1)
        sn = np.concatenate([sn, np.zeros((64, 256), np.float32)], axis=1)
        rope = np.stack([np.concatenate([cs, cs], 0), np.concatenate([sn, sn], 0)], axis=1)
        m["rope"] = np.ascontiguousarray(rope)
        am = np.zeros((128, 3, 384), np.float32)
        for v in range(3):
            am[:, v, 0:128] = band_prev
            am[:, v, 256:384] = band_next
        if q == 0:
            am[:, 0, 0:128] = NEG
        if q == 3:
            am[:, 2, 256:384] = NEG
        m["amask"] = am
        pm = pool_M.copy()
        pic = pool_IC.copy()
        if q != 0:
            pm[:, 0] = pool_M[:, 1]
            pic[:, 0] = pool_IC[:, 1]
        if q != 3:
            pm[:, 2] = pool_M[:, 1]
            pic[:, 2] = pool_IC[:, 1]
        m["pool_M"] = np.ascontiguousarray(np.concatenate([pm[:, :, 1], pool_M[:, 1:2, 0], pool_M[:, 1:2, 2]], axis=1))
        m["pool_IC"] = pic
        fl = np.zeros((128, 20), np.float32)
        fl[:, 0] = 1.0 if q > 0 else 0.0
        fl[:, 1] = 1.0 if q < 3 else 0.0
        for j in range(8):
            bj, qj = divmod(j, 4)
            fl[:, 2 + j] = 1.0 if (bj == b and qj < q) else 0.0
            fl[:, 10 + j] = 1.0 if (bj == b and qj > q) else 0.0
        m["flags"] = fl
        hi = np.zeros((128, 2), np.int32)
        hi[:, 0] = ((r - 1) % 8) * 256 + 128 + np.arange(128)
        hi[:, 1] = ((r + 1) % 8) * 256 + np.arange(128)
        m["halo_idx"] = hi
        maps.append(m)
    return maps


class Ring:
    def __init__(self, items):
        self.items = items
        self.i = 0

    def next(self):
        it = self.items[self.i % len(self.items)]
        self.i += 1
        return it


def build_program(NT, nlayers=DEPTH, dbg=(), stop_after=None, dbg_layer=0, LW=DEPTH, mode="fused", is_last=None):
    nc = bass.Bass("TRN2", target_bir_lowering=False)
    P = Prog()
    es = ExitStack()
    NTOK = NT * 128
    NBLK = NT // 4
    NCOL = 3 + (NT + 4) * 128
    C_HL = 1
    C_OWN = 1 + 128
    C_HR = C_OWN + NTOK
    C_CTX = C_HR + 128 + 1
    R_HL, R_HR, R_CTX = NTOK, NTOK + 128, NTOK + 256
    NROW = NTOK + 512
    L = LW

    def din(name, shape, dt=F32):
        return nc.dram_tensor(name, list(shape), dt, kind="ExternalInput")

    def dscr(name, shape, dt=F32):
        return nc.dram_tensor(name, list(shape), dt)

    def dout(name, shape, dt=F32):
        return nc.dram_tensor(name, list(shape), dt, kind="ExternalOutput")

    sb_sizes = {}

    def sb(name, shape, dt=F32):
        nb = int(np.prod(shape[1:])) * (4 if dt in (F32, I32) else 2)
        sb_sizes[name] = nb
        try:
            return es.enter_context(nc.sbuf_tensor("sb_" + name, list(shape), dt))
        except AssertionError:
            tot = 0
            for k_, v_ in sorted(sb_sizes.items(), key=lambda kv: -kv[1]):
                tot += v_
                print("  %-12s %7d" % (k_, v_))
            print("total", tot)
            raise

    def ps(name, shape, dt=F32):
        return es.enter_context(nc.psum_tensor("pp_" + name, list(shape), dt))

    x_in = din("x", [NTOK, D])
    ctx_in = din("ctx", [CTX, D])
    cvec_in = din("cvec", [128, 8, 2])
    w_mod = din("w_mod", [L, D, 6 * D])
    b_modT_in = din("b_modT", [128, L, 48])
    norm_gT_in = din("norm_gT", [128, L, 4, 8])
    w_in_fm = din("w_in_fm", [L, D, 1792])
    w_in_tm = din("w_in_tm", [L, D, 1280])
    multi = mode in ("A", "B")
    if multi:
        assert nlayers == 1 and LW == 1
    ctx_final = (mode == "fused") or bool(is_last)
    w_out = din("w_out", [L, D, D]) if mode != "A" else None
    sgu_wT_in = din("sgu_wT", [128, L, 4, 128])
    sgu_brep_in = din("sgu_brep", [128, L, 2, 128])
    conv_wT_in = din("conv_wT", [128, L, 4, 3])
    gate_brep_in = din("gate_brep", [128, L, 16])
    mnorm_grep_in = din("mnorm_grep", [128, L, 256])
    sink_rep_in = din("sink_rep", [128, L, 4])
    pool_w2_in = din("pool_w2", [128, L, 2, 64])
    pool_scaleT_in = din("pool_scaleT", [128, L, 2])
    w_ffn_in = din("w_ffn_in", [L, D, 2 * FFN_H]) if mode != "A" else None
    w_ffn_out = din("w_ffn_out", [L, FFN_H, D]) if mode != "A" else None
    halo_in = din("halo_in", [256, D]) if multi else None
    st_all = din("st_all", [8 * 128, 264]) if mode == "B" else None
    st_out = dout("st_out", [128, 264]) if mode == "A" else None
    ctx_out = dout("ctx_out", [CTX, D]) if (mode == "B" and not is_last) else None
    pool_M_in = din("pool_M", [128, 7, 4, 128])
    pool_IC_in = din("pool_IC", [128, 5, 2, 128])
    tri_f_in = din("tri_f", [128, 128])
    tri_b_in = din("tri_b", [128, 128])
    ident_in = din("ident", [128, 128])
    rope_in = din("rope", [128, 2, (NT + 4) * 128])
    amask_in = din("amask", [128, 3, 384])
    flags_in = din("flags", [128, 20])
    halo_idx_in = din("halo_idx", [128, 2], I32)
    out_d = dout("out", [NTOK, D]) if mode != "A" else None

    xbuf = dscr("xbuf", [NROW, D])
    xnT_d = dscr("xnT_d", [128, KC, NCOL], BF16)
    halo_src = dscr("halo_src", [256, D])
    halo_dst = dscr("halo_dst", [8 * 256, D])
    SW = 264
    st_src = dscr("st_src", [128, SW])
    st_dst = dscr("st_dst", [8 * 128, SW])
    k_xbuf = [Trk("xbuf%d" % j) for j in range(NBLK + 1)]
    k_xnT = [Trk("xnT%d" % j) for j in range(NBLK + 1)]
    k_halo_src, k_halo_dst, k_st_src, k_st_dst = Trk("hs"), Trk("hd"), Trk("ss"), Trk("sd")
    k_out = Trk("out")

    dbg_t = {}
    for name, shape, dt in dbg:
        dbg_t[name] = dout(name, shape, dt)
    k_dbg = Trk("dbg")

    def dma(q, out, in_, reads, writes, slow=False):
        if slow:
            return P.add(q, lambda e: e.dma_start(out=out, in_=in_, allow_slow_non_contiguous=True), reads, writes, dma=True)
        return P.add(q, lambda e: e.dma_start(out=out, in_=in_), reads, writes, dma=True)

    def mm(out, lhsT, rhs, start, stop, reads, writes):
        shp = tuple(lhsT.shape)
        tiled = shp[0] < 128 or int(np.prod(shp[1:])) < 128
        return P.add("pe", lambda e: e.matmul(out, lhsT, rhs, start=start, stop=stop), reads, writes, tiled=tiled)

    def tp(out, in_, ident, reads, writes):
        return P.add("pe", lambda e: e.transpose(out, in_, ident), reads, writes)

    def act(out, in_, func, reads, writes, bias=None, scale=None, accum_out=None):
        kw = {}
        if bias is not None:
            kw["bias"] = bias
        if scale is not None:
            kw["scale"] = scale
        if accum_out is not None:
            kw["accum_out"] = accum_out
        return P.add("act", lambda e: e.activation(out, in_, func, **kw), reads, writes)

    def tt(eng, out, in0, in1, op, reads, writes):
        return P.add(eng, lambda e: e.tensor_tensor(out, in0, in1, op), reads, writes)

    def ts(eng, out, in0, s1, s2, op0, op1, reads, writes, accum_out=None):
        if op1 is None:
            return P.add(eng, lambda e: e.tensor_scalar(out, in0, s1, None, op0), reads, writes)
        if accum_out is not None:
            return P.add(eng, lambda e: e.tensor_scalar(out, in0, s1, s2, op0, op1, accum_out), reads, writes)
        return P.add(eng, lambda e: e.tensor_scalar(out, in0, s1, s2, op0, op1), reads, writes)

    def stt(out, in0, scalar, in1, op0, op1, reads, writes):
        return P.add("dve", lambda e: e.scalar_tensor_tensor(out, in0, scalar, in1, op0, op1), reads, writes)

    def cp(eng, out, in_, reads, writes):
        if eng == "act":
            return P.add(eng, lambda e: e.copy(out, in_), reads, writes)
        return P.add(eng, lambda e: e.tensor_copy(out, in_), reads, writes)

    def memset(eng, ap, val, writes):
        return P.add(eng, lambda e: e.memset(ap, val), (), writes)

    def recip(out, in_, reads, writes):
        return P.add("dve", lambda e: e.reciprocal(out, in_), reads, writes)

    ident_f = sb("ident_f", [128, 128])
    ident_b = sb("ident_b", [128, 128], BF16)
    ones_f = sb("ones_f", [128, 128])
    tri_f = sb("tri_f", [128, 128])
    tri_b = sb("tri_b", [128, 128])
    mk_f = sb("mk_f", [128, 128], BF16)
    mk_b = sb("mk_b", [128, 128], BF16)
    flags = sb("flags", [128, 20])
    halo_idx = sb("halo_idx", [128, 2], I32)
    cvec = sb("cvec", [128, 8, 2])
    scv = sb("scv", [128, 8, 2])
    b_modT = sb("b_modT", [128, L, 48])
    norm_gT = sb("norm_gT", [128, L, 4, 8])
    amask = sb("amask", [128, 3, 384], BF16)
    pool_IC = sb("pool_IC", [128, 5, 2, 128])
    pool_M = sb("pool_M", [128, 7, 4, 128], BF16)
    zero_b = sb("zero_b", [128, 8, 1], BF16)
    k_const = Trk("const")

    for dst, src in ((ident_f, ident_in), (tri_f, tri_f_in), (tri_b, tri_b_in), (flags, flags_in),
                     (halo_idx, halo_idx_in), (cvec, cvec_in), (b_modT, b_modT_in), (norm_gT, norm_gT_in),
                     (pool_IC, pool_IC_in)):
        dma("sp", dst.ap(), src.ap(), (), (k_const,))
    dma("pool", pool_M.ap(), pool_M_in.ap(), (), (k_const,))
    dma("pool", amask.ap(), amask_in.ap(), (), (k_const,))
    cp("dve", ident_b[:], ident_f[:], (k_const,), (k_const,))
    cp("dve", mk_f[:], tri_f[:], (k_const,), (k_const,))
    cp("dve", mk_b[:], tri_b[:], (k_const,), (k_const,))
    memset("dve", ones_f[:], 1.0, (k_const,))
    memset("dve", zero_b[:], 0.0, (k_const,))
    act(scv[:], cvec[:], AF.Silu, (k_const,), (k_const,))
    for cpad in (0, C_HR + 128, C_CTX + 256):
        dma("sp", xnT_d[:, :, cpad:cpad + 1], zero_b[:], (k_const,), (k_xnT[NBLK],), slow=True)

    for j in range(NBLK):
        dma("sp", xbuf[j * 512:(j + 1) * 512, :], x_in[j * 512:(j + 1) * 512, :], (), (k_xbuf[j],))
    dma("sp", xbuf[R_CTX:R_CTX + 256, :], ctx_in.ap(), (), (k_xbuf[NBLK],))
    if not multi:
        dma("sp", halo_src[0:128, :], x_in[0:128, :], (), (k_halo_src,))
        dma("sp", halo_src[128:256, :], x_in[NTOK - 128:NTOK, :], (), (k_halo_src,))

    PS = [ps("ps%d" % i, [128, 512]) for i in range(7)]
    kPS = [Trk("ps%d" % i) for i in range(7)]
    PSB = ps("psb", [128, 1024], BF16)
    kPSB = Trk("psb")

    modT = sb("modT", [128, 48, 2])
    A0 = sb("A0", [128, 2, 8])
    A2 = sb("A2", [128, 2, 8])
    G1 = sb("G1", [128, 2, 8])
    G3 = sb("G3", [128, 2, 8])
    A0h = sb("A0h", [128, 2, 8])
    B0h = sb("B0h", [128, 2, 8])
    G1rep = sb("G1rep", [128, D])
    G3rep = sb("G3rep", [128, D])
    k_grep = Trk("grep")
    xt_ring = Ring([(sb("xt%d" % i, [128, D]), Trk("xt%d" % i)) for i in range(2)])
    xs_ring = Ring([(sb("xs%d" % i, [128, D]), Trk("xs%d" % i)) for i in range(2)])
    diag = [sb("diag%d" % i, [128, 128]) for i in range(2)]
    k_diag = [Trk("diag%d" % i) for i in range(2)]
    k_mod = Trk("mod")
    hT = sb("hT", [128, HT, 512], BF16)
    k_hT = Trk("hT")
    wmod_ring = Ring([(sb("wmod%d" % i, [128, 4, 128]), Trk("wmod%d" % i)) for i in range(2)])
    stg_ring = Ring([(sb("stg%d" % i, [128, 264]), Trk("stg%d" % i)) for i in range(2)])
    cst = sb("cst", [128, 4])
    memset("dve", cst[:, 0:1], 1.0, (k_const,))
    memset("dve", cst[:, 1:2], LN8, (k_const,))
    memset("dve", cst[0:64, 2:3], 1.0, (k_const,))
    memset("dve", cst[64:128, 2:3], 0.0, (k_const,))
    memset("dve", cst[0:64, 3:4], 0.0, (k_const,))
    memset("dve", cst[64:128, 3:4], 1.0, (k_const,))

    def phase0(l):
        if multi:
            dma("sp", xbuf[R_HL:R_HL + 256, :], halo_in.ap(), (), (k_xbuf[NBLK],))
        if not multi:
            P.add("pool", lambda e: e.collective_compute("AllGather", ALU.bypass, replica_groups=[list(range(8))],
                                                          ins=[halo_src.ap().opt()], outs=[halo_dst.ap().opt()]),
                  (k_halo_src,), (k_halo_dst,), dma=True, inc=1)
            for s in range(2):
                xh_, kxh_ = xs_ring.items[s]
                P.add("pool", lambda e, s=s, xh_=xh_: e.indirect_dma_start(
                    out=xh_[:, :], out_offset=None, in_=halo_dst[:, :],
                    in_offset=bass.IndirectOffsetOnAxis(ap=halo_idx[:, s:s + 1], axis=0)),
                    (k_halo_dst, k_const), (kxh_,), dma=True)
                dma("sp", xbuf[R_HL + s * 128:R_HL + (s + 1) * 128, :], xh_[:], (kxh_,), (k_xbuf[NBLK],))
        wv = w_mod[l].rearrange("(kc p) n -> p kc n", p=128)
        for j in range(48):
            for hf in range(2):
                wm, kwm = wmod_ring.next()
                dma("sp", wm[:], wv[:, hf * 4:(hf + 1) * 4, j * 128:(j + 1) * 128], (), (kwm,))
                for k4 in range(4):
                    kc = hf * 4 + k4
                    mm(PS[0][:, 2 * j:2 * j + 2], wm[:, k4, :], scv[:, kc, :], kc == 0, kc == 7, (kwm, k_const), (kPS[0],))
        psv = PS[0][:, 0:96].rearrange("p (j w) -> p j w", w=2)
        for w in range(2):
            tt("dve", modT[:, :, w], psv[:, :, w], b_modT[:, l, :], ALU.add, (kPS[0], k_const), (k_mod,))
        for w in range(2):
            stt(A0[:, w, :], modT[:, 8:16, w], 1.0, norm_gT[:, l, 0, :], ALU.add, ALU.mult, (k_mod, k_const), (k_mod,))
            stt(A2[:, w, :], modT[:, 32:40, w], 1.0, norm_gT[:, l, 2, :], ALU.add, ALU.mult, (k_mod, k_const), (k_mod,))
            tt("dve", G1[:, w, :], modT[:, 16:24, w], norm_gT[:, l, 1, :], ALU.mult, (k_mod, k_const), (k_mod,))
            tt("dve", G3[:, w, :], modT[:, 40:48, w], norm_gT[:, l, 3, :], ALU.mult, (k_mod, k_const), (k_mod,))
        for s in range(2):
            ts("dve", A0h[:, s, :], A0[:, 0, :], flags[:, s:s + 1], None, ALU.mult, None, (k_mod, k_const), (k_mod,))
            ts("dve", B0h[:, s, :], modT[:, 0:8, 0], flags[:, s:s + 1], None, ALU.mult, None, (k_mod, k_const), (k_mod,))

    def set_greps(w):
        n = 0
        for (vec, rep_) in ((G1, G1rep), (G3, G3rep)):
            for half in range(2):
                pb = 1 + (n % 2)
                n += 1
                for kk in range(4):
                    kc = half * 4 + kk
                    dg, kdg = diag[kc % 2], k_diag[kc % 2]
                    ts("dve", dg[:], ident_f[:], vec[:, w, kc:kc + 1], None, ALU.mult, None, (k_mod, k_const), (kdg,))
                    mm(PS[pb][:, kk * 128:(kk + 1) * 128], ones_f[:], dg[:], True, True, (kdg, k_const), (kPS[pb],))
                cp("act", rep_[:, half * 512:(half + 1) * 512], PS[pb][:], (kPS[pb],), (k_grep,))

    junk = sb("junk", [128, D], BF16)
    k_junk = Trk("junk")
    stat = sb("stat", [128, 16])
    k_stat = Trk("stat")
    xnb_ring = Ring([(sb("xnb%d" % i, [128, 8, 512], BF16), Trk("xnb%d" % i)) for i in range(1)])

    def rms_rstd(src_ap, col, reads):
        act(junk[:], src_ap, AF.Square, reads + (k_junk,), (k_junk, k_stat), accum_out=stat[:, col:col + 1])
        ts("dve", stat[:, col:col + 1], stat[:, col:col + 1], 1.0 / D, EPS, ALU.mult, ALU.add, (k_stat,), (k_stat,))
        act(stat[:, col:col + 1], stat[:, col:col + 1], AF.Sqrt, (k_stat,), (k_stat,))
        recip(stat[:, col:col + 1], stat[:, col:col + 1], (k_stat,), (k_stat,))

    def norm_transpose(src, ksrc, A_ap, B_ap, dst, kdst, dcol):
        rms_rstd(src, 8, (ksrc,))
        xs, kxs = xs_ring.next()
        ts("dve", xs[:], src, stat[:, 8:9], None, ALU.mult, None, (ksrc, k_stat), (kxs,))
        for kc in range(8):
            bank = 4 + kc // 4
            co = (kc % 4) * 128
            tp(PS[bank][:, co:co + 128], xs[:, kc * 128:(kc + 1) * 128], ident_f[:], (kxs, k_const), (kPS[bank],))
        for kc in range(8):
            bank = 4 + kc // 4
            co = (kc % 4) * 128
            d_ = dst[:, kc, dcol:dcol + 128]
            if kc % 2 == 0:
                ts("dve", d_, PS[bank][:, co:co + 128], A_ap(kc), B_ap(kc), ALU.mult, ALU.add, (kPS[bank], k_mod), (kdst,))
            else:
                act(d_, PS[bank][:, co:co + 128], AF.Identity, (kPS[bank], k_mod), (kdst,), bias=B_ap(kc), scale=A_ap(kc))

    def phase1a(l, j):
        special = j == NBLK
        r0 = NTOK if special else j * 512
        xnb, kxnb = xnb_ring.next()
        for i in range(4):
            xt, kxt = xt_ring.next()
            dma("sp", xt[:], xbuf[r0 + i * 128:r0 + (i + 1) * 128, :], (k_xbuf[j],), (kxt,))
            if special and i < 2:
                A_ap = lambda kc, i=i: A0h[:, i, kc:kc + 1]
                B_ap = lambda kc, i=i: B0h[:, i, kc:kc + 1]
            elif special:
                A_ap = lambda kc: A0[:, 1, kc:kc + 1]
                B_ap = lambda kc: modT[:, kc, 1:2]
            else:
                A_ap = lambda kc: A0[:, 0, kc:kc + 1]
                B_ap = lambda kc: modT[:, kc, 0:1]
            norm_transpose(xt[:], kxt, A_ap, B_ap, xnb, kxnb, i * 128)
        if special:
            dma("sp", xnT_d[:, :, C_HL:C_HL + 128], xnb[:, :, 0:128], (kxnb,), (k_xnT[j],))
            dma("sp", xnT_d[:, :, C_HR:C_HR + 128], xnb[:, :, 128:256], (kxnb,), (k_xnT[j],))
            dma("sp", xnT_d[:, :, C_CTX:C_CTX + 256], xnb[:, :, 256:512], (kxnb,), (k_xnT[j],))
        else:
            c0 = C_OWN + j * 512
            dma("sp", xnT_d[:, :, c0:c0 + 512], xnb[:], (kxnb,), (k_xnT[j],))


    wfm_s = [dscr("wfm_s%d" % p, [14, 128, 8, 128], BF16) for p in range(2)]
    wtm_s = [dscr("wtm_s%d" % p, [10, 128, 8, 128], BF16) for p in range(2)]
    wout_s = [dscr("wout_s%d" % p, [8, 128, 8, 128], BF16) for p in range(2)]
    wfi_s = [dscr("wfi_s%d" % p, [44, 128, 8, 128], BF16) for p in range(2)]
    wfo_s = [dscr("wfo_s%d" % p, [22, 128, 1024], BF16) for p in range(2)]
    k_wfm = [[Trk("wfm%d_%d" % (p, c)) for c in range(14)] for p in range(2)]
    k_wtm = [[Trk("wtm%d_%d" % (p, c)) for c in range(10)] for p in range(2)]
    k_wout = [[Trk("wout%d_%d" % (p, c)) for c in range(8)] for p in range(2)]
    k_wfi = [[Trk("wfi%d_%d" % (p, c)) for c in range(44)] for p in range(2)]
    k_wfo = [[Trk("wfo%d_%d" % (p, c)) for c in range(22)] for p in range(2)]

    def convert_weights(l):
        par = l % 2
        for (src, dst, nch, kt) in ((w_in_fm, wfm_s, 14, k_wfm), (w_in_tm, wtm_s, 10, k_wtm), (w_out, wout_s, 8, k_wout),
                                    (w_ffn_in, wfi_s, 44, k_wfi)):
            if src is None:
                continue
            v = src[l].rearrange("(kc p) n -> p kc n", p=128)
            for c in range(nch):
                wt, kw_ = wch_ring.next()
                dma("pool", wt[:], v[:, :, c * 128:(c + 1) * 128], (), (kw_,))
                dma("sp", dst[par][c], wt[:], (kw_,), (kt[par][c],))
        if w_ffn_out is None:
            return
        v = w_ffn_out[l].rearrange("(c p) n -> c p n", p=128)
        for c in range(22):
            wt, kw_ = wch_ring.next()
            wf_ = wt[:].rearrange("p a b -> p (a b)")
            dma("pool", wf_, v[c], (), (kw_,))
            dma("sp", wfo_s[par][c], wf_, (kw_,), (k_wfo[par][c],))

    conv_wT = sb("conv_wT", [128, L, 4, 3])
    gate_brep = sb("gate_brep", [128, L, 16])
    mnorm_grep = sb("mnorm_grep", [128, L, 256])
    sink_rep = sb("sink_rep", [128, L, 4])
    pool_scaleT = sb("pool_scaleT", [128, L, 2])
    for dst, src in ((conv_wT, conv_wT_in), (gate_brep, gate_brep_in), (mnorm_grep, mnorm_grep_in),
                     (sink_rep, sink_rep_in), (pool_scaleT, pool_scaleT_in)):
        dma("sp", dst.ap(), src.ap(), (), (k_const,))


    NTT = NT + 2
    bst = sb("bst", [128, NTT, 8])
    wst = sb("wst", [128, NTT, 8])
    Bsel = sb("Bsel", [128, NTT, 2, 2])
    eBst = sb("eBst", [128, NTT, 2, 2])
    cumB = sb("cumB", [128, NTT, 2, 2])
    Ecum = sb("Ecum", [128, NTT, 2, 2])
    Sloc_d = dscr("Sloc_d", [2, NTT, 128, 130], BF16)
    k_Slocd = [[Trk("Sloc%d_%d" % (d_, i)) for i in range(NTT)] for d_ in range(2)]
    Sst_ring = Ring([(sb("Sst%d" % i, [128, 2, 65], BF16), Trk("Sst%d" % i)) for i in range(2)])
    Sld = sb("Sld", [128, 2, 2, 65], BF16)
    k_Sld = Trk("Sld")
    dSb_d = dscr("dSb_d", [NTT, 128, 130])
    k_dSbd = [Trk("dSbd%d" % i) for i in range(NTT)]
    dSb_ring = Ring([(sb("dSbo%d" % i, [128, 2, 65]), Trk("dSbo%d" % i)) for i in range(2)])
    dSbi_ring = Ring([(sb("dSbi%d" % i, [128, 2, 65]), Trk("dSbi%d" % i)) for i in range(2)])
    Sf = sb("Sf", [128, 2, 65])
    Sb_ = sb("Sb", [128, 2, 65])
    Stmp = sb("Stmp", [128, 2, 65])
    stctx = sb("stctx", [128, 2, 2, 65])
    Sin = sb("Sin", [128, 2, 2, 65])
    stpack = sb("stpack", [128, SW])
    Ej = sb("Ej", [128, 2])
    Lj = sb("Lj", [128, 2, 65])
    run2 = sb("run2", [128, 2, 2])
    k_bst, k_wst, k_Bsel = Trk("bst"), Trk("wst"), Trk("Bsel")
    k_Sf, k_Sb, k_Stmp, k_stctx, k_Sin, k_stpack, k_Ej = (Trk(n) for n in ("Sf", "Sb", "Stmp", "stctx", "Sin", "stpack", "Ej"))
    k_cum = Trk("cum")

    wres = [(sb("wres%d" % i, [128, 8, 128], BF16), Trk("wres%d" % i)) for i in range(10)]
    xw_ring = Ring([(sb("xw%d" % i, [128, 8, 130], BF16), Trk("xw%d" % i)) for i in range(2)])
    cacc_ring = Ring([(sb("cacc%d" % i, [128, 2, 128]), Trk("cacc%d" % i)) for i in range(2)])
    kT_ring = Ring([(sb("kT%d" % i, [128, 2, 128], BF16), Trk("kT%d" % i)) for i in range(2)])
    v1_ring = Ring([(sb("v1_%d" % i, [128, 4, 65], BF16), Trk("v1_%d" % i)) for i in range(2)])
    g16_ring = Ring([(sb("g16_%d" % i, [128, 32]), Trk("g16_%d" % i)) for i in range(2)])
    kw_ring = Ring([(sb("kw%d" % i, [128, 2, 4, 64], BF16), Trk("kw%d" % i)) for i in range(2)])
    for (v1t, kv1) in v1_ring.items:
        memset("dve", v1t[:, :, 64:65], 1.0, (kv1,))

    def tile_col(ti):
        return C_OWN + ti * 128 if ti < NT else C_CTX + (ti - NT) * 128

    def tile_xn_trks(ti):
        if ti >= NT:
            return (k_xnT[NBLK],)
        j = ti // 4
        tr = [k_xnT[j]]
        if ti % 4 == 0:
            tr.append(k_xnT[j - 1] if j > 0 else k_xnT[NBLK])
        if ti % 4 == 3:
            tr.append(k_xnT[j + 1] if j < NBLK - 1 else k_xnT[NBLK])
        return tuple(tr)

    def conv_silu(psb, kps, l, fts, dst, kdst, coff):
        cacc, kc_ = cacc_ring.next()
        for f, ft in enumerate(fts):
            w0 = psb[:, coff + f * 130: coff + f * 130 + 128]
            w1 = psb[:, coff + f * 130 + 1: coff + f * 130 + 129]
            w2 = psb[:, coff + f * 130 + 2: coff + f * 130 + 130]
            ts("dve", cacc[:, f, :], w0, conv_wT[:, l, ft, 0:1], None, ALU.mult, None, (kps, k_const), (kc_,))
            stt(cacc[:, f, :], w1, conv_wT[:, l, ft, 1:2], cacc[:, f, :], ALU.mult, ALU.add, (kps, k_const, kc_), (kc_,))
            stt(cacc[:, f, :], w2, conv_wT[:, l, ft, 2:3], cacc[:, f, :], ALU.mult, ALU.add, (kps, k_const, kc_), (kc_,))
        act(dst, cacc[:], AF.Silu, (kc_,), (kdst,))

    def gates_and_cumsums(l, ti, psg, kpsg, goff, pc, kpc):
        g16, kg = g16_ring.next()
        tt("dve", g16[:, 0:16], psg[:, goff:goff + 16], gate_brep[:, l, :], ALU.add, (kpsg, k_const), (kg,))
        act(g16[:, 16:24], g16[:, 8:16], AF.Exp, (kg,), (kg,), scale=-1.0)
        act(g16[:, 16:24], g16[:, 16:24], AF.Ln, (kg, k_const), (kg,), bias=cst[:, 0:1])
        ts("dve", g16[:, 24:32], g16[:, 16:24], -1.0, None, ALU.mult, None, (kg,), (kg,))
        mm(pc[:, 0:4], tri_f[:], g16[:, 24:28], True, True, (kg, k_const), (kpc,))
        mm(pc[:, 4:8], tri_b[:], g16[:, 28:32], True, True, (kg, k_const), (kpc,))
        mm(pc[:, 8:16], ones_f[:], g16[:, 24:32], True, True, (kg, k_const), (kpc,))
        cp("dve", bst[:, ti, :], pc[:, 0:8], (kpc,), (k_bst,))
        tt("dve", g16[:, 8:16], g16[:, 0:8], pc[:, 0:8], ALU.subtract, (kg, kpc), (kg,))
        act(wst[:, ti, :], g16[:, 8:16], AF.Exp, (kg, k_const), (k_wst,), bias=cst[:, 1:2])
        for hh in range(2):
            src = pc[hh * 64:(hh + 1) * 64, 8:16].rearrange("p (d r h) -> p d r h", d=2, r=2, h=2)[:, :, :, hh]
            dst = Bsel[hh * 64:(hh + 1) * 64, ti, :, :].rearrange("p r d -> p d r")
            cp("dve", dst, src, (kpc,), (k_Bsel,))
        act(eBst[:, ti, :, :], Bsel[:, ti, :, :], AF.Exp, (k_Bsel,), (k_Bsel,))

    def phase1b_weights(l):
        par = l % 2
        for f in range(2):
            dma("sp", wres[f][0][:], wfm_s[par][4 + f], (k_wfm[par][4 + f],), (wres[f][1],))
            dma("sp", wres[2 + f][0][:], wtm_s[par][6 + f], (k_wtm[par][6 + f],), (wres[2 + f][1],))
        dma("sp", wres[4][0][:], wtm_s[par][9], (k_wtm[par][9],), (wres[4][1],))

    def phase1b_tile(l, ti):
        c0 = tile_col(ti)
        xw, kxw = xw_ring.next()
        dma("sp", xw[:], xnT_d[:, :, c0 - 1:c0 + 129], tile_xn_trks(ti), (kxw,))
        for ft in range(2):
            for kc in range(8):
                mm(PS[0][:, ft * 130:(ft + 1) * 130], wres[ft][0][:, kc, :], xw[:, kc, :], kc == 0, kc == 7,
                   (wres[ft][1], kxw), (kPS[0],))
        kT, kkT = kT_ring.next()
        conv_silu(PS[0], kPS[0], l, (2, 3), kT[:], kkT, 0)
        for f in range(2):
            for kc in range(8):
                mm(PS[1][:, f * 128:(f + 1) * 128], xw[:, kc, 1:129], wres[2 + f][0][:, kc, :], kc == 0, kc == 7, (wres[2 + f][1], kxw), (kPS[1],))
        for kc in range(8):
            mm(PS[1][:, 256:272], xw[:, kc, 1:129], wres[4][0][:, kc, 0:16], kc == 0, kc == 7, (wres[4][1], kxw), (kPS[1],))
        v1, kv1 = v1_ring.next()
        cp("act", v1[:, :, 0:64], PS[1][:, 0:256].rearrange("p (h d) -> p h d", d=64), (kPS[1],), (kv1,))
        gates_and_cumsums(l, ti, PS[1], kPS[1], 256, PS[6], kPS[6])
        for ft in range(2):
            tp(PSB[:, ft * 128:(ft + 1) * 128], kT[:, ft, :], ident_b[:], (kkT, k_const), (kPSB,))
        kw, kkw = kw_ring.next()
        for d_ in range(2):
            for h in range(4):
                ts("dve", kw[:, d_, h, :], PSB[:, h * 64:(h + 1) * 64], wst[:, ti, d_ * 4 + h:d_ * 4 + h + 1], None, ALU.mult, None,
                   (kPSB, k_wst), (kkw,))
        for d_ in range(2):
            for h in range(4):
                hh, pr = h % 2, h // 2
                o0 = (pr * 2 + d_) * 65
                mm(PS[5][hh * 64:(hh + 1) * 64, o0:o0 + 65], kw[:, d_, h, :], v1[:, h, :], True, True, (kkw, kv1), (kPS[5],))
        ds = PS[5][:, 0:260].rearrange("p (r d c) -> p r d c", r=2, d=2, c=65)
        dso, kdso = dSb_ring.next()
        cp("act", dso[:], ds[:, :, 1, :], (kPS[5],), (kdso,))
        dma("sp", dSb_d[ti].rearrange("p (r c) -> p r c", r=2), dso[:], (kdso,), (k_dSbd[ti],))
        sst, ksst = Sst_ring.next()
        cp("dve", sst[:], Sf[:], (k_Sf,), (ksst,))
        dma("sp", Sloc_d[0, ti].rearrange("p (r c) -> p r c", r=2), sst[:], (ksst,), (k_Slocd[0][ti],))
        dsf, kdsf = dSbi_ring.next()
        cp("act", dsf[:], ds[:, :, 0, :], (kPS[5],), (kdsf,))
        tt("dve", Stmp[:], Sf[:], dsf[:], ALU.add, (k_Sf, kdsf), (k_Stmp,))
        for pr in range(2):
            ts("dve", Sf[:, pr, :], Stmp[:, pr, :], eBst[:, ti, pr, 0:1], None, ALU.mult, None, (k_Stmp, k_Bsel), (k_Sf,))

    def bwd_step(ti):
        sst, ksst = Sst_ring.next()
        cp("dve", sst[:], Sb_[:], (k_Sb,), (ksst,))
        dma("sp", Sloc_d[1, ti].rearrange("p (r c) -> p r c", r=2), sst[:], (ksst,), (k_Slocd[1][ti],))
        dsi, kdsi = dSbi_ring.next()
        dma("sp", dsi[:], dSb_d[ti].rearrange("p (r c) -> p r c", r=2), (k_dSbd[ti],), (kdsi,))
        tt("dve", Stmp[:], Sb_[:], dsi[:], ALU.add, (k_Sb, kdsi), (k_Stmp,))
        for pr in range(2):
            ts("dve", Sb_[:, pr, :], Stmp[:, pr, :], eBst[:, ti, pr, 1:2], None, ALU.mult, None, (k_Stmp, k_Bsel), (k_Sb,))

    def phase1b(l):
        phase1b_weights(l)
        memset("dve", Sf[:], 0.0, (k_Sf,))
        memset("dve", Sb_[:], 0.0, (k_Sb,))
        for ti in (NT, NT + 1):
            phase1b_tile(l, ti)
        for ti in (NT + 1, NT):
            bwd_step(ti)
        cp("dve", stctx[:, 0, :, :], Sf[:], (k_Sf,), (k_stctx,))
        cp("dve", stctx[:, 1, :, :], Sb_[:], (k_Sb,), (k_stctx,))
        memset("dve", Sf[:], 0.0, (k_Sf,))
        memset("dve", Sb_[:], 0.0, (k_Sb,))
        for ti in range(NT):
            phase1b_tile(l, ti)
        for ti in range(NT - 1, -1, -1):
            bwd_step(ti)
        memset("dve", run2[:], 0.0, (k_cum,))
        for ti in range(NT):
            cp("dve", cumB[:, ti, :, 0], run2[:, :, 0], (k_cum,), (k_cum,))
            tt("dve", run2[:, :, 0], run2[:, :, 0], Bsel[:, ti, :, 0], ALU.add, (k_cum, k_Bsel), (k_cum,))
        for ti in range(NT - 1, -1, -1):
            cp("dve", cumB[:, ti, :, 1], run2[:, :, 1], (k_cum,), (k_cum,))
            tt("dve", run2[:, :, 1], run2[:, :, 1], Bsel[:, ti, :, 1], ALU.add, (k_cum, k_Bsel), (k_cum,))
        act(Ecum[:, 0:NT, :, :], cumB[:, 0:NT, :, :], AF.Exp, (k_cum,), (k_cum,))
        cp("dve", stpack[:, 0:130], Sf[:].rearrange("p r c -> p (r c)"), (k_Sf,), (k_stpack,))
        cp("dve", stpack[:, 130:260], Sb_[:].rearrange("p r c -> p (r c)"), (k_Sb,), (k_stpack,))
        cp("dve", stpack[:, 260:262], run2[:, :, 0], (k_cum,), (k_stpack,))
        cp("dve", stpack[:, 262:264], run2[:, :, 1], (k_cum,), (k_stpack,))
        if mode == "A":
            dma("sp", st_out.ap(), stpack[:], (k_stpack,), (k_out,))
            return
        if mode == "fused":
            dma("sp", st_src.ap(), stpack[:], (k_stpack,), (k_st_src,))
            P.add("pool", lambda e: e.collective_compute("AllGather", ALU.bypass, replica_groups=[list(range(8))],
                                                          ins=[st_src.ap().opt()], outs=[st_dst.ap().opt()]),
                  (k_st_src,), (k_st_dst,), dma=True, inc=1)
        st_gath = st_all if mode == "B" else st_dst
        for d_ in range(2):
            cp("dve", Sin[:, d_, :, :], stctx[:, d_, :, :], (k_stctx,), (k_Sin,))
            order = range(8) if d_ == 0 else range(7, -1, -1)
            fo = 2 if d_ == 0 else 10
            for j in order:
                fj = flags[:, fo + j:fo + j + 1]
                stg, k_stg = stg_ring.next()
                dma("sp", stg[:], st_gath[j * 128:(j + 1) * 128, :], (k_st_dst,), (k_stg,))
                act(Ej[:], stg[:, 260 + 2 * d_:262 + 2 * d_], AF.Exp, (k_stg, k_const), (k_Ej,), scale=fj)
                ts("dve", Lj[:].rearrange("p r c -> p (r c)"), stg[:, d_ * 130:(d_ + 1) * 130], fj, None, ALU.mult, None,
                   (k_stg, k_const), (k_Ej,))
                for pr in range(2):
                    stt(Sin[:, d_, pr, :], Sin[:, d_, pr, :], Ej[:, pr:pr + 1], Lj[:, pr, :], ALU.mult, ALU.add, (k_Sin, k_Ej), (k_Sin,))


    wch_ring = Ring([(sb("wch%d" % i, [128, 8, 128], BF16), Trk("wch%d" % i)) for i in range(6)])
    xwin = sb("xwin", [128, 8, 770], BF16)
    k_xwin = Trk("xwin")
    ropew = sb("ropew", [128, 2, 768])
    k_ropew = Trk("ropew")
    sguW = sb("sguW", [128, 4, 128], BF16)
    sguB = sb("sguB", [128, 2, 128])
    poolW = sb("poolW", [128, 2, 64], BF16)
    k_lw = Trk("layerw")
    uT = sb("uT", [128, 2, 512], BF16)
    k_uT = Trk("uT")
    aqTz = sb("aqTz", [128, 2, 2, 512], BF16)
    k_aqT = Trk("aqT")
    memset("dve", aqTz[:], 0.0, (k_aqT,))
    qkz = sb("qkz", [128, 2, 2, 2, 128], BF16)
    k_qkz = Trk("qkz")
    poolWbd = sb("poolWbd", [128, 2, 128], BF16)
    memset("dve", poolWbd[:], 0.0, (k_lw,))
    akT = sb("akT", [128, 2, 768], BF16)
    k_akT = Trk("akT")
    akTc = sb("akTc", [128, 2, 256], BF16)
    avc = sb("avc", [128, 2, 128], BF16)
    k_ctxkv = Trk("ctxkv")
    zpool = sb("zpool", [128, 6, 256], BF16)
    k_zpool = Trk("zpool")
    av = sb("av", [128, 6, 128], BF16)
    k_av = Trk("av")
    ropetmp = [sb("ropetmp%d" % i, [128, 512]) for i in range(2)]
    k_ropetmp = [Trk("ropetmp%d" % i) for i in range(2)]
    mixT = sb("mixT", [128, 8, 512], BF16)
    k_mixT = Trk("mixT")
    sgt = ropetmp
    k_sgt = k_ropetmp
    sv = sb("sv", [128, 256])
    svq = sb("svq", [128, 256])
    vhat = sb("vhat", [128, 256], BF16)
    lnst = sb("lnst", [128, 16])
    k_sv, k_vhat, k_lnst = Trk("sv"), Trk("vhat"), Trk("lnst")
    sgtmp = sb("sgtmp", [128, 128])
    k_sgtmp = Trk("sgtmp")
    qT_ring = Ring([(sb("qT%d" % i, [128, 2, 128], BF16), Trk("qT%d" % i)) for i in range(2)])
    osig = sb("osig", [128, 256])
    k_osig = Trk("osig")
    Aft = sb("Aft", [128, 2, 4, 128], BF16)
    k_Aft = Trk("Aft")
    Sfull = sb("Sfull", [128, 2, 2, 65], BF16)
    k_Sfull = Trk("Sfull")
    rr = sb("rr", [128, 24])
    k_rr = Trk("rr")
    hsum = sb("hsum", [128, 256])
    hsq = svq
    k_hsum = Trk("hsum")
    bmix = sb("bmix", [128, 256], BF16)
    k_bmix = Trk("bmix")
    s_m = sb("s_m", [128, 640])
    k_sm = Trk("s_m")
    p_b = sb("p_b", [128, 640], BF16)
    k_pb = Trk("p_b")
    pT = sb("pT", [128, 5, 128], BF16)
    k_pT = Trk("pT")
    ast = sb("ast", [128, 8])
    k_ast = Trk("ast")
    o_n = sb("o_n", [128, 256], BF16)
    k_on = Trk("o_n")
    pdT = sb("pdT", [128, 2, 128], BF16)
    k_pdT = Trk("pdT")
    ytmp = ropetmp[0]
    k_ytmp = k_ropetmp[0]

    def load_layer_weights(l):
        dma("pool", sguW[:], sgu_wT_in[:, l, :, :], (), (k_lw,))
        dma("sp", sguB[:], sgu_brep_in[:, l, :, :], (), (k_lw,))
        dma("pool", poolW[:], pool_w2_in[:, l, :, :], (), (k_lw,))
        for pr in range(2):
            for hh in range(2):
                cp("pool", poolWbd[hh * 64:(hh + 1) * 64, pr, hh * 64:(hh + 1) * 64], poolW[hh * 64:(hh + 1) * 64, pr, :], (k_lw,), (k_lw,))

    def wchunk(scr, ktr, par, c):
        wt, kw_ = wch_ring.next()
        dma("sp", wt[:], scr[par][c], (ktr[par][c],), (kw_,))
        return wt, kw_

    def fm_proj(par, c, rhs_cols, ncols, bank):
        wt, kw_ = wchunk(wfm_s, k_wfm, par, c)
        for kc in range(8):
            mm(PS[bank][:, 0:ncols], wt[:, kc, :], xwin[:, kc, rhs_cols:rhs_cols + ncols], kc == 0, kc == 7, (kw_, k_xwin), (kPS[bank],))

    def rope_apply(c_plain, c_perm, par, rhs_cols, ncols, rcol, dst, kdst, qz=None):
        fm_proj(par, c_plain, rhs_cols, ncols, 0)
        tt("dve", ropetmp[0][:, 0:ncols], PS[0][:, 0:ncols], ropew[:, 0, rcol:rcol + ncols], ALU.mult, (kPS[0], k_ropew), (k_ropetmp[0],))
        fm_proj(par, c_perm, rhs_cols, ncols, 1)
        tt("dve", ropetmp[1][:, 0:ncols], PS[1][:, 0:ncols], ropew[:, 1, rcol:rcol + ncols], ALU.mult, (kPS[1], k_ropew), (k_ropetmp[1],))
        if qz is None:
            tt("pool", dst, ropetmp[0][:, 0:ncols], ropetmp[1][:, 0:ncols], ALU.add, (k_ropetmp[0], k_ropetmp[1]), (kdst,))
        else:
            for hh in range(2):
                tt("pool", aqTz[hh * 64:(hh + 1) * 64, qz, hh, 0:ncols], ropetmp[0][hh * 64:(hh + 1) * 64, 0:ncols],
                   ropetmp[1][hh * 64:(hh + 1) * 64, 0:ncols], ALU.add, (k_ropetmp[0], k_ropetmp[1]), (kdst,))

    def phase2_block(l, j):
        par = l % 2
        is_ctx = j == NBLK
        last = l == nlayers - 1
        w = 1 if is_ctx else 0
        if is_ctx:
            tiles = [NT, NT + 1]
            wc0 = C_CTX - 1
            nwt = 2
            own0 = 0
            rc0 = C_CTX - 2
            xtr = (k_xnT[NBLK],)
        else:
            tiles = [4 * j + i for i in range(4)]
            wc0 = C_OWN + j * 512 - 128 - 1
            nwt = 6
            own0 = 1
            rc0 = C_OWN + j * 512 - 128 - 1
            xtr = tuple({k_xnT[j], k_xnT[j - 1] if j > 0 else k_xnT[NBLK], k_xnT[j + 1] if j < NBLK - 1 else k_xnT[NBLK]})
        nown = len(tiles)
        TW = nown * 128
        WW = nwt * 128
        ocol = 1 + own0 * 128
        dma("sp", xwin[:, :, 0:WW + 2], xnT_d[:, :, wc0:wc0 + WW + 2], xtr, (k_xwin,))
        dma("sp", ropew[:, :, 0:WW], rope_in[:, :, rc0:rc0 + WW], (), (k_ropew,))
        kv_only = is_ctx and last and ctx_final

        kdst_T, kk_ = (akTc, k_ctxkv) if is_ctx else (akT, k_akT)
        for c0_ in range(0, WW, 512):
            n_ = min(512, WW - c0_)
            for g in range(2):
                rope_apply(10 + g, 12 + g, par, 1 + c0_, n_, c0_, kdst_T[:, g, c0_:c0_ + n_], kk_)
        wt8, kw8 = wchunk(wtm_s, k_wtm, par, 8)
        for p in range(nwt):
            for kc in range(8):
                mm(PS[2][:, 0:128], xwin[:, kc, 1 + p * 128:1 + (p + 1) * 128], wt8[:, kc, :], kc == 0, kc == 7, (kw8, k_xwin), (kPS[2],))
            if is_ctx:
                cp("act", avc[:, p, :], PS[2][:, 0:128], (kPS[2],), (k_ctxkv,))
            else:
                cp("act", av[:, p, :], PS[2][:, 0:128], (kPS[2],), (k_av,))
        if kv_only:
            return
        wp2 = [wchunk(wtm_s, k_wtm, par, 2), wchunk(wtm_s, k_wtm, par, 3)]
        for p in range(nwt):
            for f in range(2):
                for kc in range(8):
                    mm(PS[3][:, f * 128:(f + 1) * 128], xwin[:, kc, 1 + p * 128:1 + (p + 1) * 128], wp2[f][0][:, kc, :], kc == 0, kc == 7,
                       (wp2[f][1], k_xwin), (kPS[3],))
            cp("act", zpool[:, p, :], PS[3][:, 0:256], (kPS[3],), (k_zpool,))
        for f in range(2):
            fm_proj(par, f, ocol, TW, f)
            act(uT[:, f, 0:TW], PS[f][:, 0:TW], AF.Gelu_apprx_tanh, (kPS[f],), (k_uT,))
        for f in range(2):
            rope_apply(6 + f, 8 + f, par, ocol, TW, own0 * 128, None, k_aqT, qz=f)

        def wresident(slot, scr, ktr, c):
            wt_, kw__ = wres[slot]
            dma("sp", wt_[:], scr[par][c], (ktr[par][c],), (kw__,))
            return wt_, kw__
        wsv = [wresident(f, wtm_s, k_wtm, f) for f in range(2)]
        wmo = [wresident(2 + f, wtm_s, k_wtm, 4 + f) for f in range(4)]
        wqk = [wresident(6 + f, wfm_s, k_wfm, 2 + f) for f in range(4)]
        for i, ti in enumerate(tiles):
            p = own0 + i
            tc = 1 + p * 128
            mc = i * 128
            for f in range(2):
                for kc in range(8):
                    mm(PS[4][:, f * 128:(f + 1) * 128], xwin[:, kc, tc:tc + 128], wsv[f][0][:, kc, :], kc == 0, kc == 7, (wsv[f][1], k_xwin), (kPS[4],))
            act(sv[:], PS[4][:, 0:256], AF.Gelu_apprx_tanh, (kPS[4],), (k_sv,))
            tt("pool", svq[:], sv[:], sv[:], ALU.mult, (k_sv,), (k_sv,))
            P.add("dve", lambda e: e.reduce_sum(lnst[:, 0:4], sv[:].rearrange("p (h d) -> p h d", d=64), mybir.AxisListType.X), (k_sv,), (k_lnst,))
            P.add("dve", lambda e: e.reduce_sum(lnst[:, 4:8], svq[:].rearrange("p (h d) -> p h d", d=64), mybir.AxisListType.X), (k_sv,), (k_lnst,))
            ts("dve", lnst[:, 0:4], lnst[:, 0:4], 1.0 / 64, None, ALU.mult, None, (k_lnst,), (k_lnst,))
            tt("dve", lnst[:, 8:12], lnst[:, 0:4], lnst[:, 0:4], ALU.mult, (k_lnst,), (k_lnst,))
            stt(lnst[:, 4:8], lnst[:, 4:8], 1.0 / 64, lnst[:, 8:12], ALU.mult, ALU.subtract, (k_lnst,), (k_lnst,))
            ts("dve", lnst[:, 4:8], lnst[:, 4:8], EPS, None, ALU.add, None, (k_lnst,), (k_lnst,))
            act(lnst[:, 4:8], lnst[:, 4:8], AF.Sqrt, (k_lnst,), (k_lnst,))
            recip(lnst[:, 4:8], lnst[:, 4:8], (k_lnst,), (k_lnst,))
            for h in range(4):
                ts("dve", vhat[:, h * 64:(h + 1) * 64], sv[:, h * 64:(h + 1) * 64], lnst[:, h:h + 1], lnst[:, 4 + h:5 + h], ALU.subtract, ALU.mult,
                   (k_sv, k_lnst), (k_vhat,))
            for f in range(2):
                for hh in range(2):
                    h = 2 * f + hh
                    mm(PS[5][hh * 64:(hh + 1) * 64, f * 128:(f + 1) * 128], vhat[:, h * 64:(h + 1) * 64], sguW[:, h, :], True, True,
                       (k_vhat, k_lw), (kPS[5],))
            for f in range(2):
                tt("dve", sgtmp[:], PS[5][:, f * 128:(f + 1) * 128], sguB[:, f, :], ALU.add, (kPS[5], k_lw), (k_sgtmp,))
                tt("pool", mixT[:, f, mc:mc + 128], sgtmp[:], uT[:, f, mc:mc + 128], ALU.mult, (k_sgtmp, k_uT), (k_mixT,))

            qT, kqT = qT_ring.next()
            kT, kkT = kT_ring.next()
            for half, (dstT, kd) in enumerate(((qT, kqT), (kT, kkT))):
                for f in range(2):
                    wt, kw_ = wqk[half * 2 + f]
                    for kc in range(8):
                        mm(PS[6][:, f * 130:(f + 1) * 130], wt[:, kc, :], xwin[:, kc, tc - 1:tc + 129], kc == 0, kc == 7, (kw_, k_xwin), (kPS[6],))
                conv_silu(PS[6], kPS[6], l, (2 * half, 2 * half + 1), dstT[:], kd, 0)
            for qk_, (srcT, ksrc) in enumerate(((qT, kqT), (kT, kkT))):
                for f in range(2):
                    for hh in range(2):
                        ts("pool", qkz[:, qk_, f, hh, :], srcT[:, f, :], cst[:, 2 + hh:3 + hh], None, ALU.mult, None, (ksrc, k_const), (k_qkz,))
            for f in range(4):
                for kc in range(8):
                    mm(PS[4][:, f * 128:(f + 1) * 128], xwin[:, kc, tc:tc + 128], wmo[f][0][:, kc, :], kc == 0, kc == 7, (wmo[f][1], k_xwin), (kPS[4],))
            act(osig[:], PS[4][:, 0:256], AF.Sigmoid, (kPS[4],), (k_osig,))
            v1, kv1 = v1_ring.next()
            cp("act", v1[:, :, 0:64], PS[4][:, 256:512].rearrange("p (h d) -> p h d", d=64), (kPS[4],), (kv1,))
            for h in range(4):
                hh, f = h % 2, h // 2
                mm(PS[5][:, h * 128:(h + 1) * 128], qkz[:, 1, f, hh, :], qT[:, f, :], True, True, (k_qkz, kqT), (kPS[5],))
            for d_ in range(2):
                mk = mk_f if d_ == 0 else mk_b
                for h in range(4):
                    stt(Aft[:, d_, h, :], PS[5][:, h * 128:(h + 1) * 128], wst[:, ti, d_ * 4 + h:d_ * 4 + h + 1], mk[:], ALU.mult, ALU.mult,
                        (kPS[5], k_wst, k_const), (k_Aft,))
            for d_ in range(2):
                dma("sp", Sld[:, d_, :, :], Sloc_d[d_, ti].rearrange("p (r c) -> p r c", r=2), (k_Slocd[d_][ti],), (k_Sld,))
            for d_ in range(2):
                for pr in range(2):
                    if is_ctx:
                        cp("pool", Sfull[:, pr, d_, :], Sld[:, d_, pr, :], (k_Sld,), (k_Sfull,))
                    else:
                        stt(Sfull[:, pr, d_, :], Sin[:, d_, pr, :], Ecum[:, ti, pr, d_:d_ + 1], Sld[:, d_, pr, :], ALU.mult, ALU.add,
                            (k_Sin, k_cum, k_Sld), (k_Sfull,))
            for d_ in range(2):
                bank = 0 + d_
                for h in range(4):
                    hh, f = h % 2, h // 2
                    mm(PS[bank][:, h * 65:(h + 1) * 65], Aft[:, d_, h, :], v1[:, h, :], True, False, (k_Aft, kv1), (kPS[bank],))
                    mm(PS[bank][:, h * 65:(h + 1) * 65], qkz[:, 0, f, hh, :], Sfull[:, f, d_, :], False, True,
                       (k_qkz, k_Sfull), (kPS[bank],))
            act(rr[:, 0:8], bst[:, ti, :], AF.Exp, (k_bst,), (k_rr,), scale=-1.0)
            for d_ in range(2):
                dcol = PS[d_][:, 0:260].rearrange("p (h c) -> p h c", c=65)[:, :, 64]
                act(rr[:, 8 + d_ * 4:12 + d_ * 4], dcol, AF.Abs, (kPS[d_],), (k_rr,))
            tt("dve", rr[:, 16:24], rr[:, 8:16], rr[:, 0:8], ALU.max, (k_rr,), (k_rr,))
            recip(rr[:, 16:24], rr[:, 16:24], (k_rr,), (k_rr,))
            for h in range(4):
                ts("dve", hsum[:, h * 64:(h + 1) * 64], PS[0][:, h * 65:h * 65 + 64], rr[:, 16 + h:17 + h], None, ALU.mult, None, (kPS[0], k_rr), (k_hsum,))
                stt(hsum[:, h * 64:(h + 1) * 64], PS[1][:, h * 65:h * 65 + 64], rr[:, 20 + h:21 + h], hsum[:, h * 64:(h + 1) * 64], ALU.mult, ALU.add,
                    (kPS[1], k_rr, k_hsum), (k_hsum,))
            tt("pool", hsq[:], hsum[:], hsum[:], ALU.mult, (k_hsum,), (k_hsum,))
            P.add("dve", lambda e: e.reduce_sum(lnst[:, 0:4], hsum[:].rearrange("p (h d) -> p h d", d=64), mybir.AxisListType.X), (k_hsum,), (k_lnst,))
            P.add("dve", lambda e: e.reduce_sum(lnst[:, 4:8], hsq[:].rearrange("p (h d) -> p h d", d=64), mybir.AxisListType.X), (k_hsum,), (k_lnst,))
            ts("dve", lnst[:, 0:4], lnst[:, 0:4], 1.0 / 64, None, ALU.mult, None, (k_lnst,), (k_lnst,))
            tt("dve", lnst[:, 8:12], lnst[:, 0:4], lnst[:, 0:4], ALU.mult, (k_lnst,), (k_lnst,))
            stt(lnst[:, 4:8], lnst[:, 4:8], 1.0 / 64, lnst[:, 8:12], ALU.mult, ALU.subtract, (k_lnst,), (k_lnst,))
            ts("dve", lnst[:, 4:8], lnst[:, 4:8], EPS, None, ALU.add, None, (k_lnst,), (k_lnst,))
            act(lnst[:, 4:8], lnst[:, 4:8], AF.Sqrt, (k_lnst,), (k_lnst,))
            recip(lnst[:, 4:8], lnst[:, 4:8], (k_lnst,), (k_lnst,))
            for h in range(4):
                ts("dve", hsum[:, h * 64:(h + 1) * 64], hsum[:, h * 64:(h + 1) * 64], lnst[:, h:h + 1], lnst[:, 4 + h:5 + h], ALU.subtract, ALU.mult,
                   (k_hsum, k_lnst), (k_hsum,))
            tt("pool", hsum[:], hsum[:], mnorm_grep[:, l, :], ALU.mult, (k_hsum, k_const), (k_hsum,))
            tt("pool", bmix[:], hsum[:], osig[:], ALU.mult, (k_hsum, k_osig), (k_bmix,))
            for f in range(2):
                tp(PSB[:, f * 128:(f + 1) * 128], bmix[:, f * 128:(f + 1) * 128], ident_b[:], (k_bmix, k_const), (kPSB,))
            cp("act", mixT[:, 2:4, mc:mc + 128], PSB[:, 0:256].rearrange("p (f t) -> p f t", f=2), (kPSB,), (k_mixT,))

            var = 0 if ti == 0 else (2 if ti == NT - 1 else 1)
            for h in range(4):
                hh, f, g = h % 2, h // 2, h // 2
                qh = aqTz[:, f, hh, mc:mc + 128]
                if is_ctx:
                    nk = 256
                    mm(PS[5][:, 0:256], qh, akTc[:, g, 0:256], True, True, (k_aqT, k_ctxkv), (kPS[5],))
                    ts("dve", s_m[:, 0:256], PS[5][:, 0:256], 0.125, None, ALU.mult, None, (kPS[5],), (k_sm,))
                else:
                    nk = 640
                    kc0 = (p - 1) * 128
                    mm(PS[5][:, 0:384], qh, akT[:, g, kc0:kc0 + 384], True, True, (k_aqT, k_akT), (kPS[5],))
                    mm(PS[6][:, 0:256], qh, akTc[:, g, 0:256], True, True, (k_aqT, k_ctxkv), (kPS[6],))
                    stt(s_m[:, 0:384], PS[5][:, 0:384], 0.125, amask[:, var, :], ALU.mult, ALU.add, (kPS[5], k_const), (k_sm,))
                    ts("dve", s_m[:, 384:640], PS[6][:, 0:256], 0.125, None, ALU.mult, None, (kPS[6],), (k_sm,))
                P.add("dve", lambda e, nk=nk: e.reduce_max(ast[:, 0:1], s_m[:, 0:nk], mybir.AxisListType.X), (k_sm,), (k_ast,))
                tt("dve", ast[:, 0:1], ast[:, 0:1], sink_rep[:, l, h:h + 1], ALU.max, (k_ast, k_const), (k_ast,))
                ts("dve", ast[:, 1:2], ast[:, 0:1], -1.0, None, ALU.mult, None, (k_ast,), (k_ast,))
                act(p_b[:, 0:nk], s_m[:, 0:nk], AF.Exp, (k_sm, k_ast), (k_pb, k_ast), bias=ast[:, 1:2], accum_out=ast[:, 2:3])
                act(ast[:, 3:4], sink_rep[:, l, h:h + 1], AF.Exp, (k_ast, k_const), (k_ast,), bias=ast[:, 1:2])
                tt("dve", ast[:, 4:5], ast[:, 2:3], ast[:, 3:4], ALU.add, (k_ast,), (k_ast,))
                recip(ast[:, 5:6], ast[:, 4:5], (k_ast,), (k_ast,))
                nkb = nk // 128
                for kb in range(nkb):
                    tp(PSB[:, kb * 128:(kb + 1) * 128], p_b[:, kb * 128:(kb + 1) * 128], ident_b[:], (k_pb, k_const), (kPSB,))
                cp("act", pT[:, 0:nkb, :], PSB[:, 0:nk].rearrange("p (k t) -> p k t", t=128), (kPSB,), (k_pT,))
                for kb in range(nkb):
                    if is_ctx:
                        vblk = avc[:, kb, g * 64:(g + 1) * 64]
                        vtr = k_ctxkv
                    elif kb < 3:
                        vblk = av[:, p - 1 + kb, g * 64:(g + 1) * 64]
                        vtr = k_av
                    else:
                        vblk = avc[:, kb - 3, g * 64:(g + 1) * 64]
                        vtr = k_ctxkv
                    mm(PS[4][:, h * 64:(h + 1) * 64], pT[:, kb, :], vblk, kb == 0, kb == nkb - 1, (k_pT, vtr), (kPS[4],))
                ts("dve", o_n[:, h * 64:(h + 1) * 64], PS[4][:, h * 64:(h + 1) * 64], ast[:, 5:6], None, ALU.mult, None, (kPS[4], k_ast), (k_on,))
            for f in range(2):
                tp(PSB[:, f * 128:(f + 1) * 128], o_n[:, f * 128:(f + 1) * 128], ident_b[:], (k_on, k_const), (kPSB,))
            cp("act", mixT[:, 4:6, mc:mc + 128], PSB[:, 0:256].rearrange("p (f t) -> p f t", f=2), (kPSB,), (k_mixT,))

            pv = (3 if ti == NT else 4) if is_ctx else (0 if ti == 0 else (2 if ti == NT - 1 else 1))
            for g in range(4):
                hh, pr = g % 2, g // 2
                nbs = []
                if not (is_ctx and i == 0):
                    nbs.append((p - 1, 5))
                nbs.append((p, pv))
                if not (is_ctx and i == nown - 1):
                    nbs.append((p + 1, 6))
                for n_i, (pp, mv) in enumerate(nbs):
                    mm(PS[5][hh * 64:(hh + 1) * 64, pr * 128:(pr + 1) * 128], zpool[:, pp, g * 64:(g + 1) * 64], pool_M[:, mv, g, :],
                       n_i == 0, n_i == len(nbs) - 1, (k_zpool, k_const), (kPS[5],))
            tt("dve", pdT[:], PS[5][:, 0:256].rearrange("p (r t) -> p r t", r=2), pool_IC[:, pv, :, :], ALU.mult, (kPS[5], k_const), (k_pdT,))
            for pr in range(2):
                mm(PS[6][:, pr * 128:(pr + 1) * 128], poolWbd[:, pr, :], pdT[:, pr, :], True, True, (k_lw, k_pdT), (kPS[6],))
            for pr in range(2):
                ts("dve", mixT[:, 6 + pr, mc:mc + 128], PS[6][:, pr * 128:(pr + 1) * 128], pool_scaleT[:, l, pr:pr + 1], None, ALU.mult, None,
                   (kPS[6], k_const), (k_mixT,))

        if "dbg_mixT" in dbg_t and l == dbg_layer:
            if is_ctx:
                dma("sp", dbg_t["dbg_mixT"][:, :, NTOK:NTOK + 256], mixT[:, :, 0:256], (k_mixT,), (k_dbg,))
            else:
                dma("sp", dbg_t["dbg_mixT"][:, :, j * 512:(j + 1) * 512], mixT[:], (k_mixT,), (k_dbg,))
        if stop_after == "mix":
            return

        r0 = R_CTX if is_ctx else j * 512
        xn2, kxn2 = xnb_ring.next()
        for pair in range(nown // 2):
            for c in range(8):
                wt, kw_ = wchunk(wout_s, k_wout, par, c)
                for tl in range(2):
                    i = pair * 2 + tl
                    bank = 2 * tl + c // 4
                    for kc in range(8):
                        mm(PS[bank][:, (c % 4) * 128:(c % 4 + 1) * 128], mixT[:, kc, i * 128:(i + 1) * 128], wt[:, kc, :], kc == 0, kc == 7,
                           (k_mixT, kw_), (kPS[bank],))
            for tl in range(2):
                i = pair * 2 + tl
                xt, kxt = xt_ring.next()
                dma("sp", xt[:], xbuf[r0 + i * 128:r0 + (i + 1) * 128, :], (k_xbuf[j],), (kxt,))
                residual(PS[2 * tl], kPS[2 * tl], PS[2 * tl + 1], kPS[2 * tl + 1], xt, kxt, G1rep)
                dma("sp", xbuf[r0 + i * 128:r0 + (i + 1) * 128, :], xt[:], (kxt,), (k_xbuf[j],))
                norm_transpose(xt[:], kxt, lambda kc: A2[:, w, kc:kc + 1], lambda kc: modT[:, 24 + kc, w:w + 1], xn2, kxn2, i * 128)
        for ht in range(HT):
            wg_, kwg = wchunk(wfi_s, k_wfi, par, ht)
            wu_, kwu = wchunk(wfi_s, k_wfi, par, HT + ht)
            bg = 0 + (ht % 2) * 2
            bu = 1 + (ht % 2) * 2
            for kc in range(8):
                mm(PS[bg][:, 0:TW], wg_[:, kc, :], xn2[:, kc, 0:TW], kc == 0, kc == 7, (kwg, kxn2), (kPS[bg],))
            for kc in range(8):
                mm(PS[bu][:, 0:TW], wu_[:, kc, :], xn2[:, kc, 0:TW], kc == 0, kc == 7, (kwu, kxn2), (kPS[bu],))
            sg, ksg = sgt[ht % 2], k_sgt[ht % 2]
            act(sg[:, 0:TW], PS[bg][:, 0:TW], AF.Silu, (kPS[bg],), (ksg,))
            tt("dve", hT[:, ht, 0:TW], sg[:, 0:TW], PS[bu][:, 0:TW], ALU.mult, (ksg, kPS[bu]), (k_hT,))
        for pair in range(nown // 2):
            for hk in range(HT):
                wt, kw_ = wch_ring.next()
                dma("sp", wt[:].rearrange("p a b -> p (a b)"), wfo_s[par][hk], (k_wfo[par][hk],), (kw_,))
                wv_ = wt[:].rearrange("p a b -> p (a b)")
                for tl in range(2):
                    i = pair * 2 + tl
                    for half in range(2):
                        bank = 2 * tl + half
                        mm(PS[bank][:, 0:512], hT[:, hk, i * 128:(i + 1) * 128], wv_[:, half * 512:(half + 1) * 512], hk == 0, hk == HT - 1,
                           (k_hT, kw_), (kPS[bank],))
            for tl in range(2):
                i = pair * 2 + tl
                ti = tiles[i]
                xt, kxt = xt_ring.next()
                dma("sp", xt[:], xbuf[r0 + i * 128:r0 + (i + 1) * 128, :], (k_xbuf[j],), (kxt,))
                residual(PS[2 * tl], kPS[2 * tl], PS[2 * tl + 1], kPS[2 * tl + 1], xt, kxt, G3rep)
                if is_ctx:
                    dma("sp", xbuf[R_CTX + i * 128:R_CTX + (i + 1) * 128, :], xt[:], (kxt,), (k_xbuf[NBLK],))
                    if ctx_out is not None:
                        dma("sp", ctx_out[i * 128:(i + 1) * 128, :], xt[:], (kxt,), (k_out,))
                elif last:
                    dma("sp", out_d[ti * 128:(ti + 1) * 128, :], xt[:], (kxt,), (k_out,))
                else:
                    dma("sp", xbuf[ti * 128:(ti + 1) * 128, :], xt[:], (kxt,), (k_xbuf[j],))
                    if ti == 0:
                        dma("sp", halo_src[0:128, :], xt[:], (kxt,), (k_halo_src,))
                    if ti == NT - 1:
                        dma("sp", halo_src[128:256, :], xt[:], (kxt,), (k_halo_src,))

    def residual(psA, kA, psB, kB, xt, kxt, Grep_):
        act(junk[:, 0:512], psA[:, 0:512], AF.Square, (kA, k_junk), (k_junk, k_stat), accum_out=stat[:, 9:10])
        act(junk[:, 512:1024], psB[:, 0:512], AF.Square, (kB, k_junk), (k_junk, k_stat), accum_out=stat[:, 10:11])
        tt("dve", stat[:, 9:10], stat[:, 9:10], stat[:, 10:11], ALU.add, (k_stat,), (k_stat,))
        ts("dve", stat[:, 9:10], stat[:, 9:10], 1.0 / D, EPS, ALU.mult, ALU.add, (k_stat,), (k_stat,))
        act(stat[:, 9:10], stat[:, 9:10], AF.Sqrt, (k_stat,), (k_stat,))
        recip(stat[:, 9:10], stat[:, 9:10], (k_stat,), (k_stat,))
        for half, (pp_, kp_) in enumerate(((psA, kA), (psB, kB))):
            stt(ytmp[:], pp_[:, 0:512], stat[:, 9:10], Grep_[:, half * 512:(half + 1) * 512], ALU.mult, ALU.mult, (kp_, k_stat, k_grep), (k_ytmp,))
            tt("pool", xt[:, half * 512:(half + 1) * 512], xt[:, half * 512:(half + 1) * 512], ytmp[:], ALU.add, (kxt, k_ytmp), (kxt,))

    convert_weights(0)
    for l in range(nlayers):
        phase0(l)
        phase1a(l, NBLK)
        for j in range(NBLK):
            phase1a(l, j)
        if stop_after == "1a":
            break
        load_layer_weights(l)
        phase1b(l)
        if mode == "A":
            break
        if l + 1 < nlayers:
            convert_weights(l + 1)
        if stop_after == "1b":
            break
        if not (l == nlayers - 1 and ctx_final) and stop_after != "mix":
            set_greps(1)
        phase2_block(l, NBLK)
        if stop_after != "mix":
            set_greps(0)
        for j in range(NBLK):
            phase2_block(l, j)

    if "dbg_xnT" in dbg_t:
        dma("sp", dbg_t["dbg_xnT"].ap(), xnT_d.ap(), tuple(k_xnT), (k_dbg,))
    if "dbg_mod" in dbg_t:
        dma("sp", dbg_t["dbg_mod"].ap(), modT[:], (k_mod,), (k_dbg,))
    if "dbg_g1rep" in dbg_t:
        dma("sp", dbg_t["dbg_g1rep"].ap(), G1rep[:], (k_mod,), (k_dbg,))

    print("sbuf bytes remaining per partition:", nc.sbuf_bytes_remaining() if callable(nc.sbuf_bytes_remaining) else nc.sbuf_bytes_remaining)
    if MAXOPS:
        P.ops = P.ops[:MAXOPS]
    nops = P.emit(nc, es)
    es.close()
    return nc, nops


_PROGRAM_CACHE = {}
_WKEYS = ("w_mod", "b_mod", "norm_g", "w_in", "w_out", "sgu_w", "sgu_b", "mlstm_conv_w", "mlstm_gate_b", "mlstm_norm_g",
          "attn_sink", "pool_w", "pool_scale", "w_ffn_in", "w_ffn_out")
_A_SKIP = ("w_out", "w_ffn_in", "w_ffn_out")


def _get_program(NT, mode, is_last=None):
    key = (NT, mode, is_last)
    if key not in _PROGRAM_CACHE:
        if mode == "fused":
            _PROGRAM_CACHE[key] = build_program(NT, nlayers=DEPTH, LW=DEPTH)[0]
        else:
            _PROGRAM_CACHE[key] = build_program(NT, nlayers=1, LW=1, mode=mode, is_last=is_last)[0]
    return _PROGRAM_CACHE[key]


def _layer_maps(inputs, l, x_cur, ctx_cur, NT):
    li = {k: np.asarray(inputs[k])[l:l + 1] for k in _WKEYS}
    li.update(x=x_cur, ctx=ctx_cur, c=np.asarray(inputs["c"]), c_ctx=np.asarray(inputs["c_ctx"]))
    maps = _host_prep(li, NT)
    NTOK = NT * 128
    N = x_cur.shape[1]
    for r in range(8):
        b, q = divmod(r, 4)
        halo = np.zeros((256, D), np.float32)
        if q > 0:
            halo[0:128] = x_cur[b, q * NTOK - 128:q * NTOK]
        if q < 3:
            halo[128:256] = x_cur[b, (q + 1) * NTOK:(q + 1) * NTOK + 128]
        maps[r]["halo_in"] = halo
    return maps


def kernel_multi(runner=None, **inputs):
    run = runner or (lambda nc, maps: run_bass_kernel_spmd(nc, maps, core_ids=list(range(8))).results)
    x_cur = np.asarray(inputs["x"], np.float32)
    ctx_cur = np.asarray(inputs["ctx"], np.float32)
    B, N, _ = x_cur.shape
    NT = N // (4 * 128)
    NTOK = NT * 128
    for l in range(DEPTH):
        is_last = l == DEPTH - 1
        maps = _layer_maps(inputs, l, x_cur, ctx_cur, NT)
        mapsA = [{k: v for k, v in m.items() if k not in _A_SKIP} for m in maps]
        resA = run(_get_program(NT, "A"), mapsA)
        st_all = np.ascontiguousarray(np.concatenate([np.asarray(resA[r]["st_out"], np.float32) for r in range(8)], axis=0))
        for m in maps:
            m["st_all"] = st_all
        resB = run(_get_program(NT, "B", is_last), maps)
        x_new = np.empty_like(x_cur)
        for r in range(8):
            b, q = divmod(r, 4)
            x_new[b, q * NTOK:(q + 1) * NTOK] = np.asarray(resB[r]["out"], np.float32)
        if not is_last:
            ctx_cur = np.stack([np.asarray(resB[0]["ctx_out"], np.float32), np.asarray(resB[4]["ctx_out"], np.float32)], axis=0)
        x_cur = x_new
    return x_cur


def kernel_fused(**inputs):
    x = np.asarray(inputs["x"])
    B, N, _ = x.shape
    NT = N // (4 * 128)
    maps = _host_prep(inputs, NT)
    nc = _get_program(NT, "fused")
    res = run_bass_kernel_spmd(nc, maps, core_ids=list(range(8)))
    NTOK = NT * 128
    out = np.empty((B, N, D), np.float32)
    for r in range(8):
        b, q = divmod(r, 4)
        out[b, q * NTOK:(q + 1) * NTOK] = np.asarray(res.results[r]["out"], np.float32)
    return out


def kernel(**inputs):
    return kernel_multi(**inputs)
```

```python
import math
from contextlib import ExitStack

import numpy as np
import concourse.bass as bass
import concourse.mybir as mybir
from concourse.bass_utils import run_bass_kernel_spmd

F32 = mybir.dt.float32
BF16 = mybir.dt.bfloat16
I32 = mybir.dt.int32
AF = mybir.ActivationFunctionType
ALU = mybir.AluOpType

D = 1024
KC = 8
DEPTH = 4
CTX = 256
GRID_W = 64
EPS = 1e-6
FFN_H = 2816
HT = FFN_H // 128
NEG = -30000.0
LN8 = math.log(0.125)
MAXOPS = None


class Trk:
    __slots__ = ("name", "w", "r")

    def __init__(self, name):
        self.name = name
        self.w = None
        self.r = []


class Op:
    __slots__ = ("eng", "fn", "deps", "dma", "sem", "val", "sig", "inc", "tiled")

    def __init__(self, eng, fn, deps, dma):
        self.eng = eng
        self.fn = fn
        self.deps = deps
        self.dma = dma
        self.sem = None
        self.val = 0
        self.sig = False
        self.inc = None
        self.tiled = False


ENGS = ("pe", "act", "dve", "pool", "sp")


class Prog:
    def __init__(self):
        self.ops = []

    def add(self, eng, fn, reads=(), writes=(), dma=False, inc=None, tiled=False):
        i = len(self.ops)
        deps = set()
        for t in reads:
            if t.w is not None:
                deps.add(t.w)
        for t in writes:
            if t.w is not None:
                deps.add(t.w)
            deps.update(t.r)
        deps.discard(i)
        for t in reads:
            t.r.append(i)
        for t in writes:
            t.w = i
            t.r = []
        op = Op(eng, fn, deps, dma)
        op.inc = inc
        op.tiled = tiled
        self.ops.append(op)
        return i

    def emit(self, nc, es, n_dma_sems=40):
        ops = self.ops
        for op in ops:
            if op.dma:
                op.sig = True
        for op in ops:
            for d in op.deps:
                dop = ops[d]
                if dop.eng == "pe" and op.eng == "pe" and not dop.dma and not op.dma:
                    continue
                dop.sig = True
        esem = {e: es.enter_context(nc.semaphore("s_" + e)) for e in ENGS}
        dsems = [es.enter_context(nc.semaphore("d%d" % k)) for k in range(n_dma_sems)]
        dcount = [0] * n_dma_sems
        dprev = [None] * n_dma_sems
        ecount = {e: 0 for e in ENGS}
        extra = {}
        n_sp = n_dma_sems - 14
        pools = {"sp": list(range(0, n_sp)), "pool": list(range(n_sp, n_dma_sems - 2)), "cc": list(range(n_dma_sems - 2, n_dma_sems))}
        cnt = {"sp": 0, "pool": 0, "cc": 0}
        for i, op in enumerate(ops):
            if op.dma:
                kind = "cc" if op.inc == 1 else ("pool" if op.eng == "pool" else "sp")
                lst = pools[kind]
                k = lst[cnt[kind] % len(lst)]
                cnt[kind] += 1
                op.sem = dsems[k]
                if op.inc is None:
                    op.inc = 16
                dcount[k] += op.inc
                op.val = dcount[k]
                if dprev[k] is not None:
                    extra[i] = dprev[k]
                dprev[k] = i
            elif op.sig:
                ecount[op.eng] += 1
                op.sem = esem[op.eng]
                op.inc = 1
                op.val = ecount[op.eng]
        streams = {e: [] for e in ENGS}
        for i, op in enumerate(ops):
            streams[op.eng].append(i)
        final_waits = [(dsems[k], dcount[k]) for k in range(n_dma_sems) if dcount[k] > 0]

        def run(engname, eng):
            seen = {}
            pe_mode = False
            for i in streams[engname]:
                op = ops[i]
                if engname == "pe" and not op.dma and op.tiled != pe_mode:
                    eng.drain()
                    pe_mode = op.tiled
                deps = set(op.deps)
                if i in extra:
                    deps.add(extra[i])
                need = {}
                for d in deps:
                    dop = ops[d]
                    if not dop.sig:
                        continue
                    if dop.eng == "pe" and engname == "pe" and not dop.dma and not op.dma:
                        continue
                    key = id(dop.sem)
                    if seen.get(key, 0) >= dop.val:
                        continue
                    if key not in need or need[key][1] < dop.val:
                        need[key] = (dop.sem, dop.val)
                for key, (sem, val) in need.items():
                    eng.wait_ge(sem, val)
                    seen[key] = val
                ins = op.fn(eng)
                if op.sig:
                    ins.then_inc(op.sem, op.inc)
            if engname == "sp":
                for sem, val in final_waits:
                    eng.wait_ge(sem, val)

        with nc.Block() as block:
            block.tensor(lambda e: run("pe", e))
            block.scalar(lambda e: run("act", e))
            block.vector(lambda e: run("dve", e))
            block.gpsimd(lambda e: run("pool", e))
            block.sync(lambda e: run("sp", e))
        return len(ops)


G = 256
OFF_M = 512
OFF_A = OFF_M + 4 * G + 16
OFF_P = OFF_A + 512
ROPE_PERM = None


def _rope_perm():
    perm = np.zeros(64, np.int64)
    sgn = np.zeros(64, np.float32)
    for d in range(64):
        g32, o = divmod(d, 32)
        if o < 16:
            perm[d] = g32 * 32 + o + 16
            sgn[d] = -1.0
        else:
            perm[d] = g32 * 32 + o - 16
            sgn[d] = 1.0
    return perm, sgn


def _rope_tables(pos):
    pos = np.asarray(pos)
    row = (pos // GRID_W).astype(np.float32)
    col = (pos % GRID_W).astype(np.float32)
    inv = np.power(np.float32(10000.0), -np.arange(16, dtype=np.float32) * np.float32(2.0) / np.float32(32.0)).astype(np.float32)
    ar = row[:, None] * inv
    ac = col[:, None] * inv
    ang = np.concatenate([ar, ar, ac, ac], axis=-1).astype(np.float32)
    _, sgn = _rope_perm()
    return np.cos(ang).T.astype(np.float32), (np.sin(ang) * sgn[None, :]).T.astype(np.float32)


def _pool_tables():
    wins = (2, 4, 8, 16)

    def build(first, last, n_total_tiles_hint):
        M = np.zeros((3, 4, 128, 128), np.float32)
        IC = np.zeros((4, 128), np.float32)
        for g, w in enumerate(wins):
            for t in range(128):
                lo = t - w // 2
                hi = t + w - w // 2
                if first:
                    lo = max(lo, 0)
                if last:
                    hi = min(hi, 128)
                cnt = hi - lo
                IC[g, t] = 1.0 / cnt
                for s in range(lo, hi):
                    if s < 0:
                        M[0, g, s + 128, t] += 1.0
                    elif s >= 128:
                        M[2, g, s - 128, t] += 1.0
                    else:
                        M[1, g, s, t] += 1.0
                M[1, g, t, t] -= cnt
        return M, IC

    out = [build(True, False, 0), build(False, False, 0), build(False, True, 0), build(True, False, 0), build(False, True, 0)]
    return out


def _host_prep(inputs, NT):
    x = np.asarray(inputs["x"], np.float32)
    B, N, _ = x.shape
    assert N == 4 * NT * 128
    NTOK = NT * 128
    c = np.asarray(inputs["c"], np.float32)
    ctx = np.asarray(inputs["ctx"], np.float32)
    c_ctx = np.asarray(inputs["c_ctx"], np.float32)
    w_in = np.asarray(inputs["w_in"], np.float32)
    perm, _ = _rope_perm()
    L = w_in.shape[0]

    def fmcols():
        cols = []
        cols += list(range(0, 256))
        cols += list(range(OFF_M, OFF_M + 256))
        cols += list(range(OFF_M + 256, OFF_M + 512))
        aq = [OFF_A + h * 64 + d for h in range(4) for d in range(64)]
        aqp = [OFF_A + h * 64 + perm[d] for h in range(4) for d in range(64)]
        ak = [OFF_A + 256 + g * 64 + d for g in range(2) for _dup in range(2) for d in range(64)]
        akp = [OFF_A + 256 + g * 64 + perm[d] for g in range(2) for _dup in range(2) for d in range(64)]
        cols += aq + aqp + ak + akp
        return np.array(cols)

    def tmcols():
        cols = []
        cols += list(range(256, 512))
        cols += list(range(OFF_P, OFF_P + 256))
        cols += list(range(OFF_M + 768, OFF_M + 1024))
        cols += list(range(OFF_M + 512, OFF_M + 768))
        cols += list(range(OFF_A + 384, OFF_A + 512))
        gbase = OFF_M + 1024
        for gt in (0, 2, 1, 3):
            cols += [gbase + gt * 4 + h for h in range(4)]
        return np.array(cols)

    w_in_fm = np.ascontiguousarray(w_in[:, :, fmcols()])
    w_in_tm = np.zeros((L, D, 1280), np.float32)
    w_in_tm[:, :, :1168] = w_in[:, :, tmcols()]
    b_mod = np.asarray(inputs["b_mod"], np.float32)
    b_modT = np.ascontiguousarray(b_mod.reshape(L, 48, 128).transpose(2, 0, 1))
    norm_g = np.asarray(inputs["norm_g"], np.float32)
    norm_gT = np.ascontiguousarray(norm_g.reshape(L, 4, 8, 128).transpose(3, 0, 1, 2))
    sgu_w = np.asarray(inputs["sgu_w"], np.float32)
    sgu_wT = np.ascontiguousarray(sgu_w.transpose(3, 0, 1, 2))
    sgu_b = np.asarray(inputs["sgu_b"], np.float32)
    sgu_brep = np.zeros((128, L, 2, 128), np.float32)
    for ft in range(2):
        for hh in range(2):
            sgu_brep[hh * 64:(hh + 1) * 64, :, ft, :] = sgu_b[None, :, 2 * ft + hh, :]
    conv_w = np.asarray(inputs["mlstm_conv_w"], np.float32)
    conv_wT = np.ascontiguousarray(conv_w.reshape(L, 3, 4, 128).transpose(3, 0, 2, 1))
    gate_b = np.asarray(inputs["mlstm_gate_b"], np.float32)
    gb = np.concatenate([gate_b[:, 0], gate_b[:, 2], gate_b[:, 1], gate_b[:, 3]], axis=-1)
    gate_brep = np.ascontiguousarray(np.broadcast_to(gb[None], (128, L, 16)))
    mng = np.asarray(inputs["mlstm_norm_g"], np.float32)
    mnorm_grep = np.ascontiguousarray(np.broadcast_to(mng[None], (128, L, 256)))
    sink = np.asarray(inputs["attn_sink"], np.float32)
    sink_rep = np.ascontiguousarray(np.broadcast_to(sink[None], (128, L, 4)))
    pool_w = np.asarray(inputs["pool_w"], np.float32)
    pool_w2 = np.zeros((128, L, 2, 64), np.float32)
    for pr in range(2):
        for hh in range(2):
            pool_w2[hh * 64:(hh + 1) * 64, :, pr, :] = pool_w[:, 2 * pr + hh].transpose(1, 0, 2)
    pool_scale = np.asarray(inputs["pool_scale"], np.float32)
    pool_scaleT = np.ascontiguousarray(pool_scale.reshape(L, 2, 128).transpose(2, 0, 1))

    ptab = _pool_tables()
    pool_M = np.zeros((128, 5, 3, 4, 128), np.float32)
    pool_IC = np.zeros((128, 5, 2, 128), np.float32)
    for v in range(5):
        M, IC = ptab[v]
        pool_M[:, v] = M.transpose(2, 0, 1, 3)
        for pr in range(2):
            for hh in range(2):
                pool_IC[hh * 64:(hh + 1) * 64, v, pr, :] = IC[2 * pr + hh][None, :]
    tt = np.arange(128)
    band_prev = np.where(tt[None, :] >= tt[:, None], 0.0, NEG).astype(np.float32)
    band_next = np.where(tt[None, :] <= tt[:, None], 0.0, NEG).astype(np.float32)
    tri_f = (tt[:, None] <= tt[None, :]).astype(np.float32)
    tri_b = (tt[:, None] >= tt[None, :]).astype(np.float32)

    shared = dict(
        w_mod=np.asarray(inputs["w_mod"], np.float32), b_modT=b_modT, norm_gT=norm_gT,
        w_in_fm=w_in_fm, w_in_tm=w_in_tm, w_out=np.asarray(inputs["w_out"], np.float32),
        sgu_wT=sgu_wT, sgu_brep=sgu_brep, conv_wT=conv_wT, gate_brep=gate_brep, mnorm_grep=mnorm_grep,
        sink_rep=sink_rep, pool_w2=pool_w2, pool_scaleT=pool_scaleT,
        w_ffn_in=np.asarray(inputs["w_ffn_in"], np.float32), w_ffn_out=np.asarray(inputs["w_ffn_out"], np.float32),
        tri_f=tri_f, tri_b=tri_b, ident=np.eye(128, dtype=np.float32),
    )
    maps = []
    for r in range(8):
        b, q = divmod(r, 4)
        m = dict(shared)
        m["x"] = np.ascontiguousarray(x[b, q * NTOK:(q + 1) * NTOK])
        m["ctx"] = np.ascontiguousarray(ctx[b])
        cv = np.stack([c[b], c_ctx], axis=-1)
        m["cvec"] = np.ascontiguousarray(cv.reshape(8, 128, 2).transpose(1, 0, 2))
        pos = np.concatenate([np.arange(q * NTOK - 128, q * NTOK), np.arange(q * NTOK, (q + 1) * NTOK),
                              np.arange((q + 1) * NTOK, (q + 1) * NTOK + 128)])
        pos = np.clip(pos, 0, N - 1)
        cs, sn = _rope_tables(pos)
        cs = np.concatenate([cs, np.ones((64, 256), np.float32)], axis=1)
        sn = np.concatenate([sn, np.zeros((64, 256), np.float32)], axis=1)
        rope = np.stack([np.concatenate([cs, cs], 0), np.concatenate([sn, sn], 0)], axis=1)
        m["rope"] = np.ascontiguousarray(rope)
        am = np.zeros((128, 3, 384), np.float32)
        for v in range(3):
            am[:, v, 0:128] = band_prev
            am[:, v, 256:384] = band_next
        if q == 0:
            am[:, 0, 0:128] = NEG
        if q == 3:
            am[:, 2, 256:384] = NEG
        m["amask"] = am
        pm = pool_M.copy()
        pic = pool_IC.copy()
        if q != 0:
            pm[:, 0] = pool_M[:, 1]
            pic[:, 0] = pool_IC[:, 1]
        if q != 3:
            pm[:, 2] = pool_M[:, 1]
            pic[:, 2] = pool_IC[:, 1]
        m["pool_M"] = np.ascontiguousarray(np.concatenate([pm[:, :, 1], pool_M[:, 1:2, 0], pool_M[:, 1:2, 2]], axis=1))
        m["pool_IC"] = pic
        fl = np.zeros((128, 20), np.float32)
        fl[:, 0] = 1.0 if q > 0 else 0.0
        fl[:, 1] = 1.0 if q < 3 else 0.0
        for j in range(8):
            bj, qj = divmod(j, 4)
            fl[:, 2 + j] = 1.0 if (bj == b and qj < q) else 0.0
            fl[:, 10 + j] = 1.0 if (bj == b and qj > q) else 0.0
        m["flags"] = fl
        hi = np.zeros((128, 2), np.int32)
        hi[:, 0] = ((r - 1) % 8) * 256 + 128 + np.arange(128)
        hi[:, 1] = ((r + 1) % 8) * 256 + np.arange(128)
        m["halo_idx"] = hi
        maps.append(m)
    return maps


class Ring:
    def __init__(self, items):
        self.items = items
        self.i = 0

    def next(self):
        it = self.items[self.i % len(self.items)]
        self.i += 1
        return it


def build_program(NT, nlayers=DEPTH, dbg=(), stop_after=None, dbg_layer=0, LW=DEPTH, mode="fused", is_last=None):
    nc = bass.Bass("TRN2", target_bir_lowering=False)
    P = Prog()
    es = ExitStack()
    NTOK = NT * 128
    NBLK = NT // 4
    NCOL = 3 + (NT + 4) * 128
    C_HL = 1
    C_OWN = 1 + 128
    C_HR = C_OWN + NTOK
    C_CTX = C_HR + 128 + 1
    R_HL, R_HR, R_CTX = NTOK, NTOK + 128, NTOK + 256
    NROW = NTOK + 512
    L = LW

    def din(name, shape, dt=F32):
        return nc.dram_tensor(name, list(shape), dt, kind="ExternalInput")

    def dscr(name, shape, dt=F32):
        return nc.dram_tensor(name, list(shape), dt)

    def dout(name, shape, dt=F32):
        return nc.dram_tensor(name, list(shape), dt, kind="ExternalOutput")

    sb_sizes = {}

    def sb(name, shape, dt=F32):
        nb = int(np.prod(shape[1:])) * (4 if dt in (F32, I32) else 2)
        sb_sizes[name] = nb
        try:
            return es.enter_context(nc.sbuf_tensor("sb_" + name, list(shape), dt))
        except AssertionError:
            tot = 0
            for k_, v_ in sorted(sb_sizes.items(), key=lambda kv: -kv[1]):
                tot += v_
                print("  %-12s %7d" % (k_, v_))
            print("total", tot)
            raise

    def ps(name, shape, dt=F32):
        return es.enter_context(nc.psum_tensor("pp_" + name, list(shape), dt))

    x_in = din("x", [NTOK, D])
    ctx_in = din("ctx", [CTX, D])
    cvec_in = din("cvec", [128, 8, 2])
    w_mod = din("w_mod", [L, D, 6 * D])
    b_modT_in = din("b_modT", [128, L, 48])
    norm_gT_in = din("norm_gT", [128, L, 4, 8])
    w_in_fm = din("w_in_fm", [L, D, 1792])
    w_in_tm = din("w_in_tm", [L, D, 1280])
    multi = mode in ("A", "B")
    if multi:
        assert nlayers == 1 and LW == 1
    ctx_final = (mode == "fused") or bool(is_last)
    w_out = din("w_out", [L, D, D]) if mode != "A" else None
    sgu_wT_in = din("sgu_wT", [128, L, 4, 128])
    sgu_brep_in = din("sgu_brep", [128, L, 2, 128])
    conv_wT_in = din("conv_wT", [128, L, 4, 3])
    gate_brep_in = din("gate_brep", [128, L, 16])
    mnorm_grep_in = din("mnorm_grep", [128, L, 256])
    sink_rep_in = din("sink_rep", [128, L, 4])
    pool_w2_in = din("pool_w2", [128, L, 2, 64])
    pool_scaleT_in = din("pool_scaleT", [128, L, 2])
    w_ffn_in = din("w_ffn_in", [L, D, 2 * FFN_H]) if mode != "A" else None
    w_ffn_out = din("w_ffn_out", [L, FFN_H, D]) if mode != "A" else None
    halo_in = din("halo_in", [256, D]) if multi else None
    st_all = din("st_all", [8 * 128, 264]) if mode == "B" else None
    st_out = dout("st_out", [128, 264]) if mode == "A" else None
    ctx_out = dout("ctx_out", [CTX, D]) if (mode == "B" and not is_last) else None
    pool_M_in = din("pool_M", [128, 7, 4, 128])
    pool_IC_in = din("pool_IC", [128, 5, 2, 128])
    tri_f_in = din("tri_f", [128, 128])
    tri_b_in = din("tri_b", [128, 128])
    ident_in = din("ident", [128, 128])
    rope_in = din("rope", [128, 2, (NT + 4) * 128])
    amask_in = din("amask", [128, 3, 384])
    flags_in = din("flags", [128, 20])
    halo_idx_in = din("halo_idx", [128, 2], I32)
    out_d = dout("out", [NTOK, D]) if mode != "A" else None

    xbuf = dscr("xbuf", [NROW, D])
    xnT_d = dscr("xnT_d", [128, KC, NCOL], BF16)
    halo_src = dscr("halo_src", [256, D])
    halo_dst = dscr("halo_dst", [8 * 256, D])
    SW = 264
    st_src = dscr("st_src", [128, SW])
    st_dst = dscr("st_dst", [8 * 128, SW])
    k_xbuf = [Trk("xbuf%d" % j) for j in range(NBLK + 1)]
    k_xnT = [Trk("xnT%d" % j) for j in range(NBLK + 1)]
    k_halo_src, k_halo_dst, k_st_src, k_st_dst = Trk("hs"), Trk("hd"), Trk("ss"), Trk("sd")
    k_out = Trk("out")

    dbg_t = {}
    for name, shape, dt in dbg:
        dbg_t[name] = dout(name, shape, dt)
    k_dbg = Trk("dbg")

    def dma(q, out, in_, reads, writes, slow=False):
        if slow:
            return P.add(q, lambda e: e.dma_start(out=out, in_=in_, allow_slow_non_contiguous=True), reads, writes, dma=True)
        return P.add(q, lambda e: e.dma_start(out=out, in_=in_), reads, writes, dma=True)

    def mm(out, lhsT, rhs, start, stop, reads, writes):
        shp = tuple(lhsT.shape)
        tiled = shp[0] < 128 or int(np.prod(shp[1:])) < 128
        return P.add("pe", lambda e: e.matmul(out, lhsT, rhs, start=start, stop=stop), reads, writes, tiled=tiled)

    def tp(out, in_, ident, reads, writes):
        return P.add("pe", lambda e: e.transpose(out, in_, ident), reads, writes)

    def act(out, in_, func, reads, writes, bias=None, scale=None, accum_out=None):
        kw = {}
        if bias is not None:
            kw["bias"] = bias
        if scale is not None:
            kw["scale"] = scale
        if accum_out is not None:
            kw["accum_out"] = accum_out
        return P.add("act", lambda e: e.activation(out, in_, func, **kw), reads, writes)

    def tt(eng, out, in0, in1, op, reads, writes):
        return P.add(eng, lambda e: e.tensor_tensor(out, in0, in1, op), reads, writes)

    def ts(eng, out, in0, s1, s2, op0, op1, reads, writes, accum_out=None):
        if op1 is None:
            return P.add(eng, lambda e: e.tensor_scalar(out, in0, s1, None, op0), reads, writes)
        if accum_out is not None:
            return P.add(eng, lambda e: e.tensor_scalar(out, in0, s1, s2, op0, op1, accum_out), reads, writes)
        return P.add(eng, lambda e: e.tensor_scalar(out, in0, s1, s2, op0, op1), reads, writes)

    def stt(out, in0, scalar, in1, op0, op1, reads, writes):
        return P.add("dve", lambda e: e.scalar_tensor_tensor(out, in0, scalar, in1, op0, op1), reads, writes)

    def cp(eng, out, in_, reads, writes):
        if eng == "act":
            return P.add(eng, lambda e: e.copy(out, in_), reads, writes)
        return P.add(eng, lambda e: e.tensor_copy(out, in_), reads, writes)

    def memset(eng, ap, val, writes):
        return P.add(eng, lambda e: e.memset(ap, val), (), writes)

    def recip(out, in_, reads, writes):
        return P.add("dve", lambda e: e.reciprocal(out, in_), reads, writes)

    ident_f = sb("ident_f", [128, 128])
    ident_b = sb("ident_b", [128, 128], BF16)
    ones_f = sb("ones_f", [128, 128])
    tri_f = sb("tri_f", [128, 128])
    tri_b = sb("tri_b", [128, 128])
    mk_f = sb("mk_f", [128, 128], BF16)
    mk_b = sb("mk_b", [128, 128], BF16)
    flags = sb("flags", [128, 20])
    halo_idx = sb("halo_idx", [128, 2], I32)
    cvec = sb("cvec", [128, 8, 2])
    scv = sb("scv", [128, 8, 2])
    b_modT = sb("b_modT", [128, L, 48])
    norm_gT = sb("norm_gT", [128, L, 4, 8])
    amask = sb("amask", [128, 3, 384], BF16)
    pool_IC = sb("pool_IC", [128, 5, 2, 128])
    pool_M = sb("pool_M", [128, 7, 4, 128], BF16)
    zero_b = sb("zero_b", [128, 8, 1], BF16)
    k_const = Trk("const")

    for dst, src in ((ident_f, ident_in), (tri_f, tri_f_in), (tri_b, tri_b_in), (flags, flags_in),
                     (halo_idx, halo_idx_in), (cvec, cvec_in), (b_modT, b_modT_in), (norm_gT, norm_gT_in),
                     (pool_IC, pool_IC_in)):
        dma("sp", dst.ap(), src.ap(), (), (k_const,))
    dma("pool", pool_M.ap(), pool_M_in.ap(), (), (k_const,))
    dma("pool", amask.ap(), amask_in.ap(), (), (k_const,))
    cp("dve", ident_b[:], ident_f[:], (k_const,), (k_const,))
    cp("dve", mk_f[:], tri_f[:], (k_const,), (k_const,))
    cp("dve", mk_b[:], tri_b[:], (k_const,), (k_const,))
    memset("dve", ones_f[:], 1.0, (k_const,))
    memset("dve", zero_b[:], 0.0, (k_const,))
    act(scv[:], cvec[:], AF.Silu, (k_const,), (k_const,))
    for cpad in (0, C_HR + 128, C_CTX + 256):
        dma("sp", xnT_d[:, :, cpad:cpad + 1], zero_b[:], (k_const,), (k_xnT[NBLK],), slow=True)

    for j in range(NBLK):
        dma("sp", xbuf[j * 512:(j + 1) * 512, :], x_in[j * 512:(j + 1) * 512, :], (), (k_xbuf[j],))
    dma("sp", xbuf[R_CTX:R_CTX + 256, :], ctx_in.ap(), (), (k_xbuf[NBLK],))
    if not multi:
        dma("sp", halo_src[0:128, :], x_in[0:128, :], (), (k_halo_src,))
        dma("sp", halo_src[128:256, :], x_in[NTOK - 128:NTOK, :], (), (k_halo_src,))

    PS = [ps("ps%d" % i, [128, 512]) for i in range(7)]
    kPS = [Trk("ps%d" % i) for i in range(7)]
    PSB = ps("psb", [128, 1024], BF16)
    kPSB = Trk("psb")

    modT = sb("modT", [128, 48, 2])
    A0 = sb("A0", [128, 2, 8])
    A2 = sb("A2", [128, 2, 8])
    G1 = sb("G1", [128, 2, 8])
    G3 = sb("G3", [128, 2, 8])
    A0h = sb("A0h", [128, 2, 8])
    B0h = sb("B0h", [128, 2, 8])
    G1rep = sb("G1rep", [128, D])
    G3rep = sb("G3rep", [128, D])
    k_grep = Trk("grep")
    xt_ring = Ring([(sb("xt%d" % i, [128, D]), Trk("xt%d" % i)) for i in range(2)])
    xs_ring = Ring([(sb("xs%d" % i, [128, D]), Trk("xs%d" % i)) for i in range(2)])
    diag = [sb("diag%d" % i, [128, 128]) for i in range(2)]
    k_diag = [Trk("diag%d" % i) for i in range(2)]
    k_mod = Trk("mod")
    hT = sb("hT", [128, HT, 512], BF16)
    k_hT = Trk("hT")
    wmod_ring = Ring([(sb("wmod%d" % i, [128, 4, 128]), Trk("wmod%d" % i)) for i in range(2)])
    stg_ring = Ring([(sb("stg%d" % i, [128, 264]), Trk("stg%d" % i)) for i in range(2)])
    cst = sb("cst", [128, 4])
    memset("dve", cst[:, 0:1], 1.0, (k_const,))
    memset("dve", cst[:, 1:2], LN8, (k_const,))
    memset("dve", cst[0:64, 2:3], 1.0, (k_const,))
    memset("dve", cst[64:128, 2:3], 0.0, (k_const,))
    memset("dve", cst[0:64, 3:4], 0.0, (k_const,))
    memset("dve", cst[64:128, 3:4], 1.0, (k_const,))

    def phase0(l):
        if multi:
            dma("sp", xbuf[R_HL:R_HL + 256, :], halo_in.ap(), (), (k_xbuf[NBLK],))
        if not multi:
            P.add("pool", lambda e: e.collective_compute("AllGather", ALU.bypass, replica_groups=[list(range(8))],
                                                          ins=[halo_src.ap().opt()], outs=[halo_dst.ap().opt()]),
                  (k_halo_src,), (k_halo_dst,), dma=True, inc=1)
            for s in range(2):
                xh_, kxh_ = xs_ring.items[s]
                P.add("pool", lambda e, s=s, xh_=xh_: e.indirect_dma_start(
                    out=xh_[:, :], out_offset=None, in_=halo_dst[:, :],
                    in_offset=bass.IndirectOffsetOnAxis(ap=halo_idx[:, s:s + 1], axis=0)),
                    (k_halo_dst, k_const), (kxh_,), dma=True)
                dma("sp", xbuf[R_HL + s * 128:R_HL + (s + 1) * 128, :], xh_[:], (kxh_,), (k_xbuf[NBLK],))
        wv = w_mod[l].rearrange("(kc p) n -> p kc n", p=128)
        for j in range(48):
            for hf in range(2):
                wm, kwm = wmod_ring.next()
                dma("sp", wm[:], wv[:, hf * 4:(hf + 1) * 4, j * 128:(j + 1) * 128], (), (kwm,))
                for k4 in range(4):
                    kc = hf * 4 + k4
                    mm(PS[0][:, 2 * j:2 * j + 2], wm[:, k4, :], scv[:, kc, :], kc == 0, kc == 7, (kwm, k_const), (kPS[0],))
        psv = PS[0][:, 0:96].rearrange("p (j w) -> p j w", w=2)
        for w in range(2):
            tt("dve", modT[:, :, w], psv[:, :, w], b_modT[:, l, :], ALU.add, (kPS[0], k_const), (k_mod,))
        for w in range(2):
            stt(A0[:, w, :], modT[:, 8:16, w], 1.0, norm_gT[:, l, 0, :], ALU.add, ALU.mult, (k_mod, k_const), (k_mod,))
            stt(A2[:, w, :], modT[:, 32:40, w], 1.0, norm_gT[:, l, 2, :], ALU.add, ALU.mult, (k_mod, k_const), (k_mod,))
            tt("dve", G1[:, w, :], modT[:, 16:24, w], norm_gT[:, l, 1, :], ALU.mult, (k_mod, k_const), (k_mod,))
            tt("dve", G3[:, w, :], modT[:, 40:48, w], norm_gT[:, l, 3, :], ALU.mult, (k_mod, k_const), (k_mod,))
        for s in range(2):
            ts("dve", A0h[:, s, :], A0[:, 0, :], flags[:, s:s + 1], None, ALU.mult, None, (k_mod, k_const), (k_mod,))
            ts("dve", B0h[:, s, :], modT[:, 0:8, 0], flags[:, s:s + 1], None, ALU.mult, None, (k_mod, k_const), (k_mod,))

    def set_greps(w):
        n = 0
        for (vec, rep_) in ((G1, G1rep), (G3, G3rep)):
            for half in range(2):
                pb = 1 + (n % 2)
                n += 1
                for kk in range(4):
                    kc = half * 4 + kk
                    dg, kdg = diag[kc % 2], k_diag[kc % 2]
                    ts("dve", dg[:], ident_f[:], vec[:, w, kc:kc + 1], None, ALU.mult, None, (k_mod, k_const), (kdg,))
                    mm(PS[pb][:, kk * 128:(kk + 1) * 128], ones_f[:], dg[:], True, True, (kdg, k_const), (kPS[pb],))
                cp("act", rep_[:, half * 512:(half + 1) * 512], PS[pb][:], (kPS[pb],), (k_grep,))

    junk = sb("junk", [128, D], BF16)
    k_junk = Trk("junk")
    stat = sb("stat", [128, 16])
    k_stat = Trk("stat")
    xnb_ring = Ring([(sb("xnb%d" % i, [128, 8, 512], BF16), Trk("xnb%d" % i)) for i in range(1)])

    def rms_rstd(src_ap, col, reads):
        act(junk[:], src_ap, AF.Square, reads + (k_junk,), (k_junk, k_stat), accum_out=stat[:, col:col + 1])
        ts("dve", stat[:, col:col + 1], stat[:, col:col + 1], 1.0 / D, EPS, ALU.mult, ALU.add, (k_stat,), (k_stat,))
        act(stat[:, col:col + 1], stat[:, col:col + 1], AF.Sqrt, (k_stat,), (k_stat,))
        recip(stat[:, col:col + 1], stat[:, col:col + 1], (k_stat,), (k_stat,))

    def norm_transpose(src, ksrc, A_ap, B_ap, dst, kdst, dcol):
        rms_rstd(src, 8, (ksrc,))
        xs, kxs = xs_ring.next()
        ts("dve", xs[:], src, stat[:, 8:9], None, ALU.mult, None, (ksrc, k_stat), (kxs,))
        for kc in range(8):
            bank = 4 + kc // 4
            co = (kc % 4) * 128
            tp(PS[bank][:, co:co + 128], xs[:, kc * 128:(kc + 1) * 128], ident_f[:], (kxs, k_const), (kPS[bank],))
        for kc in range(8):
            bank = 4 + kc // 4
            co = (kc % 4) * 128
            d_ = dst[:, kc, dcol:dcol + 128]
            if kc % 2 == 0:
                ts("dve", d_, PS[bank][:, co:co + 128], A_ap(kc), B_ap(kc), ALU.mult, ALU.add, (kPS[bank], k_mod), (kdst,))
            else:
                act(d_, PS[bank][:, co:co + 128], AF.Identity, (kPS[bank], k_mod), (kdst,), bias=B_ap(kc), scale=A_ap(kc))

    def phase1a(l, j):
        special = j == NBLK
        r0 = NTOK if special else j * 512
        xnb, kxnb = xnb_ring.next()
        for i in range(4):
            xt, kxt = xt_ring.next()
            dma("sp", xt[:], xbuf[r0 + i * 128:r0 + (i + 1) * 128, :], (k_xbuf[j],), (kxt,))
            if special and i < 2:
                A_ap = lambda kc, i=i: A0h[:, i, kc:kc + 1]
                B_ap = lambda kc, i=i: B0h[:, i, kc:kc + 1]
            elif special:
                A_ap = lambda kc: A0[:, 1, kc:kc + 1]
                B_ap = lambda kc: modT[:, kc, 1:2]
            else:
                A_ap = lambda kc: A0[:, 0, kc:kc + 1]
                B_ap = lambda kc: modT[:, kc, 0:1]
            norm_transpose(xt[:], kxt, A_ap, B_ap, xnb, kxnb, i * 128)
        if special:
            dma("sp", xnT_d[:, :, C_HL:C_HL + 128], xnb[:, :, 0:128], (kxnb,), (k_xnT[j],))
            dma("sp", xnT_d[:, :, C_HR:C_HR + 128], xnb[:, :, 128:256], (kxnb,), (k_xnT[j],))
            dma("sp", xnT_d[:, :, C_CTX:C_CTX + 256], xnb[:, :, 256:512], (kxnb,), (k_xnT[j],))
        else:
            c0 = C_OWN + j * 512
            dma("sp", xnT_d[:, :, c0:c0 + 512], xnb[:], (kxnb,), (k_xnT[j],))


    wfm_s = [dscr("wfm_s%d" % p, [14, 128, 8, 128], BF16) for p in range(2)]
    wtm_s = [dscr("wtm_s%d" % p, [10, 128, 8, 128], BF16) for p in range(2)]
    wout_s = [dscr("wout_s%d" % p, [8, 128, 8, 128], BF16) for p in range(2)]
    wfi_s = [dscr("wfi_s%d" % p, [44, 128, 8, 128], BF16) for p in range(2)]
    wfo_s = [dscr("wfo_s%d" % p, [22, 128, 1024], BF16) for p in range(2)]
    k_wfm = [[Trk("wfm%d_%d" % (p, c)) for c in range(14)] for p in range(2)]
    k_wtm = [[Trk("wtm%d_%d" % (p, c)) for c in range(10)] for p in range(2)]
    k_wout = [[Trk("wout%d_%d" % (p, c)) for c in range(8)] for p in range(2)]
    k_wfi = [[Trk("wfi%d_%d" % (p, c)) for c in range(44)] for p in range(2)]
    k_wfo = [[Trk("wfo%d_%d" % (p, c)) for c in range(22)] for p in range(2)]

    def convert_weights(l):
        par = l % 2
        for (src, dst, nch, kt) in ((w_in_fm, wfm_s, 14, k_wfm), (w_in_tm, wtm_s, 10, k_wtm), (w_out, wout_s, 8, k_wout),
                                    (w_ffn_in, wfi_s, 44, k_wfi)):
            if src is None:
                continue
            v = src[l].rearrange("(kc p) n -> p kc n", p=128)
            for c in range(nch):
                wt, kw_ = wch_ring.next()
                dma("pool", wt[:], v[:, :, c * 128:(c + 1) * 128], (), (kw_,))
                dma("sp", dst[par][c], wt[:], (kw_,), (kt[par][c],))
        if w_ffn_out is None:
            return
        v = w_ffn_out[l].rearrange("(c p) n -> c p n", p=128)
        for c in range(22):
            wt, kw_ = wch_ring.next()
            wf_ = wt[:].rearrange("p a b -> p (a b)")
            dma("pool", wf_, v[c], (), (kw_,))
            dma("sp", wfo_s[par][c], wf_, (kw_,), (k_wfo[par][c],))

    conv_wT = sb("conv_wT", [128, L, 4, 3])
    gate_brep = sb("gate_brep", [128, L, 16])
    mnorm_grep = sb("mnorm_grep", [128, L, 256])
    sink_rep = sb("sink_rep", [128, L, 4])
    pool_scaleT = sb("pool_scaleT", [128, L, 2])
    for dst, src in ((conv_wT, conv_wT_in), (gate_brep, gate_brep_in), (mnorm_grep, mnorm_grep_in),
                     (sink_rep, sink_rep_in), (pool_scaleT, pool_scaleT_in)):
        dma("sp", dst.ap(), src.ap(), (), (k_const,))


    NTT = NT + 2
    bst = sb("bst", [128, NTT, 8])
    wst = sb("wst", [128, NTT, 8])
    Bsel = sb("Bsel", [128, NTT, 2, 2])
    eBst = sb("eBst", [128, NTT, 2, 2])
    cumB = sb("cumB", [128, NTT, 2, 2])
    Ecum = sb("Ecum", [128, NTT, 2, 2])
    Sloc_d = dscr("Sloc_d", [2, NTT, 128, 130], BF16)
    k_Slocd = [[Trk("Sloc%d_%d" % (d_, i)) for i in range(NTT)] for d_ in range(2)]
    Sst_ring = Ring([(sb("Sst%d" % i, [128, 2, 65], BF16), Trk("Sst%d" % i)) for i in range(2)])
    Sld = sb("Sld", [128, 2, 2, 65], BF16)
    k_Sld = Trk("Sld")
    dSb_d = dscr("dSb_d", [NTT, 128, 130])
    k_dSbd = [Trk("dSbd%d" % i) for i in range(NTT)]
    dSb_ring = Ring([(sb("dSbo%d" % i, [128, 2, 65]), Trk("dSbo%d" % i)) for i in range(2)])
    dSbi_ring = Ring([(sb("dSbi%d" % i, [128, 2, 65]), Trk("dSbi%d" % i)) for i in range(2)])
    Sf = sb("Sf", [128, 2, 65])
    Sb_ = sb("Sb", [128, 2, 65])
    Stmp = sb("Stmp", [128, 2, 65])
    stctx = sb("stctx", [128, 2, 2, 65])
    Sin = sb("Sin", [128, 2, 2, 65])
    stpack = sb("stpack", [128, SW])
    Ej = sb("Ej", [128, 2])
    Lj = sb("Lj", [128, 2, 65])
    run2 = sb("run2", [128, 2, 2])
    k_bst, k_wst, k_Bsel = Trk("bst"), Trk("wst"), Trk("Bsel")
    k_Sf, k_Sb, k_Stmp, k_stctx, k_Sin, k_stpack, k_Ej = (Trk(n) for n in ("Sf", "Sb", "Stmp", "stctx", "Sin", "stpack", "Ej"))
    k_cum = Trk("cum")

    wres = [(sb("wres%d" % i, [128, 8, 128], BF16), Trk("wres%d" % i)) for i in range(10)]
    xw_ring = Ring([(sb("xw%d" % i, [128, 8, 130], BF16), Trk("xw%d" % i)) for i in range(2)])
    cacc_ring = Ring([(sb("cacc%d" % i, [128, 2, 128]), Trk("cacc%d" % i)) for i in range(2)])
    kT_ring = Ring([(sb("kT%d" % i, [128, 2, 128], BF16), Trk("kT%d" % i)) for i in range(2)])
    v1_ring = Ring([(sb("v1_%d" % i, [128, 4, 65], BF16), Trk("v1_%d" % i)) for i in range(2)])
    g16_ring = Ring([(sb("g16_%d" % i, [128, 32]), Trk("g16_%d" % i)) for i in range(2)])
    kw_ring = Ring([(sb("kw%d" % i, [128, 2, 4, 64], BF16), Trk("kw%d" % i)) for i in range(2)])
    for (v1t, kv1) in v1_ring.items:
        memset("dve", v1t[:, :, 64:65], 1.0, (kv1,))

    def tile_col(ti):
        return C_OWN + ti * 128 if ti < NT else C_CTX + (ti - NT) * 128

    def tile_xn_trks(ti):
        if ti >= NT:
            return (k_xnT[NBLK],)
        j = ti // 4
        tr = [k_xnT[j]]
        if ti % 4 == 0:
            tr.append(k_xnT[j - 1] if j > 0 else k_xnT[NBLK])
        if ti % 4 == 3:
            tr.append(k_xnT[j + 1] if j < NBLK - 1 else k_xnT[NBLK])
        return tuple(tr)

    def conv_silu(psb, kps, l, fts, dst, kdst, coff):
        cacc, kc_ = cacc_ring.next()
        for f, ft in enumerate(fts):
            w0 = psb[:, coff + f * 130: coff + f * 130 + 128]
            w1 = psb[:, coff + f * 130 + 1: coff + f * 130 + 129]
            w2 = psb[:, coff + f * 130 + 2: coff + f * 130 + 130]
            ts("dve", cacc[:, f, :], w0, conv_wT[:, l, ft, 0:1], None, ALU.mult, None, (kps, k_const), (kc_,))
            stt(cacc[:, f, :], w1, conv_wT[:, l, ft, 1:2], cacc[:, f, :], ALU.mult, ALU.add, (kps, k_const, kc_), (kc_,))
            stt(cacc[:, f, :], w2, conv_wT[:, l, ft, 2:3], cacc[:, f, :], ALU.mult, ALU.add, (kps, k_const, kc_), (kc_,))
        act(dst, cacc[:], AF.Silu, (kc_,), (kdst,))

    def gates_and_cumsums(l, ti, psg, kpsg, goff, pc, kpc):
        g16, kg = g16_ring.next()
        tt("dve", g16[:, 0:16], psg[:, goff:goff + 16], gate_brep[:, l, :], ALU.add, (kpsg, k_const), (kg,))
        act(g16[:, 16:24], g16[:, 8:16], AF.Exp, (kg,), (kg,), scale=-1.0)
        act(g16[:, 16:24], g16[:, 16:24], AF.Ln, (kg, k_const), (kg,), bias=cst[:, 0:1])
        ts("dve", g16[:, 24:32], g16[:, 16:24], -1.0, None, ALU.mult, None, (kg,), (kg,))
        mm(pc[:, 0:4], tri_f[:], g16[:, 24:28], True, True, (kg, k_const), (kpc,))
        mm(pc[:, 4:8], tri_b[:], g16[:, 28:32], True, True, (kg, k_const), (kpc,))
        mm(pc[:, 8:16], ones_f[:], g16[:, 24:32], True, True, (kg, k_const), (kpc,))
        cp("dve", bst[:, ti, :], pc[:, 0:8], (kpc,), (k_bst,))
        tt("dve", g16[:, 8:16], g16[:, 0:8], pc[:, 0:8], ALU.subtract, (kg, kpc), (kg,))
        act(wst[:, ti, :], g16[:, 8:16], AF.Exp, (kg, k_const), (k_wst,), bias=cst[:, 1:2])
        for hh in range(2):
            src = pc[hh * 64:(hh + 1) * 64, 8:16].rearrange("p (d r h) -> p d r h", d=2, r=2, h=2)[:, :, :, hh]
            dst = Bsel[hh * 64:(hh + 1) * 64, ti, :, :].rearrange("p r d -> p d r")
            cp("dve", dst, src, (kpc,), (k_Bsel,))
        act(eBst[:, ti, :, :], Bsel[:, ti, :, :], AF.Exp, (k_Bsel,), (k_Bsel,))

    def phase1b_weights(l):
        par = l % 2
        for f in range(2):
            dma("sp", wres[f][0][:], wfm_s[par][4 + f], (k_wfm[par][4 + f],), (wres[f][1],))
            dma("sp", wres[2 + f][0][:], wtm_s[par][6 + f], (k_wtm[par][6 + f],), (wres[2 + f][1],))
        dma("sp", wres[4][0][:], wtm_s[par][9], (k_wtm[par][9],), (wres[4][1],))

    def phase1b_tile(l, ti):
        c0 = tile_col(ti)
        xw, kxw = xw_ring.next()
        dma("sp", xw[:], xnT_d[:, :, c0 - 1:c0 + 129], tile_xn_trks(ti), (kxw,))
        for ft in range(2):
            for kc in range(8):
                mm(PS[0][:, ft * 130:(ft + 1) * 130], wres[ft][0][:, kc, :], xw[:, kc, :], kc == 0, kc == 7,
                   (wres[ft][1], kxw), (kPS[0],))
        kT, kkT = kT_ring.next()
        conv_silu(PS[0], kPS[0], l, (2, 3), kT[:], kkT, 0)
        for f in range(2):
            for kc in range(8):
                mm(PS[1][:, f * 128:(f + 1) * 128], xw[:, kc, 1:129], wres[2 + f][0][:, kc, :], kc == 0, kc == 7, (wres[2 + f][1], kxw), (kPS[1],))
        for kc in range(8):
            mm(PS[1][:, 256:272], xw[:, kc, 1:129], wres[4][0][:, kc, 0:16], kc == 0, kc == 7, (wres[4][1], kxw), (kPS[1],))
        v1, kv1 = v1_ring.next()
        cp("act", v1[:, :, 0:64], PS[1][:, 0:256].rearrange("p (h d) -> p h d", d=64), (kPS[1],), (kv1,))
        gates_and_cumsums(l, ti, PS[1], kPS[1], 256, PS[6], kPS[6])
        for ft in range(2):
            tp(PSB[:, ft * 128:(ft + 1) * 128], kT[:, ft, :], ident_b[:], (kkT, k_const), (kPSB,))
        kw, kkw = kw_ring.next()
        for d_ in range(2):
            for h in range(4):
                ts("dve", kw[:, d_, h, :], PSB[:, h * 64:(h + 1) * 64], wst[:, ti, d_ * 4 + h:d_ * 4 + h + 1], None, ALU.mult, None,
                   (kPSB, k_wst), (kkw,))
        for d_ in range(2):
            for h in range(4):
                hh, pr = h % 2, h // 2
                o0 = (pr * 2 + d_) * 65
                mm(PS[5][hh * 64:(hh + 1) * 64, o0:o0 + 65], kw[:, d_, h, :], v1[:, h, :], True, True, (kkw, kv1), (kPS[5],))
        ds = PS[5][:, 0:260].rearrange("p (r d c) -> p r d c", r=2, d=2, c=65)
        dso, kdso = dSb_ring.next()
        cp("act", dso[:], ds[:, :, 1, :], (kPS[5],), (kdso,))
        dma("sp", dSb_d[ti].rearrange("p (r c) -> p r c", r=2), dso[:], (kdso,), (k_dSbd[ti],))
        sst, ksst = Sst_ring.next()
        cp("dve", sst[:], Sf[:], (k_Sf,), (ksst,))
        dma("sp", Sloc_d[0, ti].rearrange("p (r c) -> p r c", r=2), sst[:], (ksst,), (k_Slocd[0][ti],))
        dsf, kdsf = dSbi_ring.next()
        cp("act", dsf[:], ds[:, :, 0, :], (kPS[5],), (kdsf,))
        tt("dve", Stmp[:], Sf[:], dsf[:], ALU.add, (k_Sf, kdsf), (k_Stmp,))
        for pr in range(2):
            ts("dve", Sf[:, pr, :], Stmp[:, pr, :], eBst[:, ti, pr, 0:1], None, ALU.mult, None, (k_Stmp, k_Bsel), (k_Sf,))

    def bwd_step(ti):
        sst, ksst = Sst_ring.next()
        cp("dve", sst[:], Sb_[:], (k_Sb,), (ksst,))
        dma("sp", Sloc_d[1, ti].rearrange("p (r c) -> p r c", r=2), sst[:], (ksst,), (k_Slocd[1][ti],))
        dsi, kdsi = dSbi_ring.next()
        dma("sp", dsi[:], dSb_d[ti].rearrange("p (r c) -> p r c", r=2), (k_dSbd[ti],), (kdsi,))
        tt("dve", Stmp[:], Sb_[:], dsi[:], ALU.add, (k_Sb, kdsi), (k_Stmp,))
        for pr in range(2):
            ts("dve", Sb_[:, pr, :], Stmp[:, pr, :], eBst[:, ti, pr, 1:2], None, ALU.mult, None, (k_Stmp, k_Bsel), (k_Sb,))

    def phase1b(l):
        phase1b_weights(l)
        memset("dve", Sf[:], 0.0, (k_Sf,))
        memset("dve", Sb_[:], 0.0, (k_Sb,))
        for ti in (NT, NT + 1):
            phase1b_tile(l, ti)
        for ti in (NT + 1, NT):
            bwd_step(ti)
        cp("dve", stctx[:, 0, :, :], Sf[:], (k_Sf,), (k_stctx,))
        cp("dve", stctx[:, 1, :, :], Sb_[:], (k_Sb,), (k_stctx,))
        memset("dve", Sf[:], 0.0, (k_Sf,))
        memset("dve", Sb_[:], 0.0, (k_Sb,))
        for ti in range(NT):
            phase1b_tile(l, ti)
        for ti in range(NT - 1, -1, -1):
            bwd_step(ti)
        memset("dve", run2[:], 0.0, (k_cum,))
        for ti in range(NT):
            cp("dve", cumB[:, ti, :, 0], run2[:, :, 0], (k_cum,), (k_cum,))
            tt("dve", run2[:, :, 0], run2[:, :, 0], Bsel[:, ti, :, 0], ALU.add, (k_cum, k_Bsel), (k_cum,))
        for ti in range(NT - 1, -1, -1):
            cp("dve", cumB[:, ti, :, 1], run2[:, :, 1], (k_cum,), (k_cum,))
            tt("dve", run2[:, :, 1], run2[:, :, 1], Bsel[:, ti, :, 1], ALU.add, (k_cum, k_Bsel), (k_cum,))
        act(Ecum[:, 0:NT, :, :], cumB[:, 0:NT, :, :], AF.Exp, (k_cum,), (k_cum,))
        cp("dve", stpack[:, 0:130], Sf[:].rearrange("p r c -> p (r c)"), (k_Sf,), (k_stpack,))
        cp("dve", stpack[:, 130:260], Sb_[:].rearrange("p r c -> p (r c)"), (k_Sb,), (k_stpack,))
        cp("dve", stpack[:, 260:262], run2[:, :, 0], (k_cum,), (k_stpack,))
        cp("dve", stpack[:, 262:264], run2[:, :, 1], (k_cum,), (k_stpack,))
        if mode == "A":
            dma("sp", st_out.ap(), stpack[:], (k_stpack,), (k_out,))
            return
        if mode == "fused":
            dma("sp", st_src.ap(), stpack[:], (k_stpack,), (k_st_src,))
            P.add("pool", lambda e: e.collective_compute("AllGather", ALU.bypass, replica_groups=[list(range(8))],
                                                          ins=[st_src.ap().opt()], outs=[st_dst.ap().opt()]),
                  (k_st_src,), (k_st_dst,), dma=True, inc=1)
        st_gath = st_all if mode == "B" else st_dst
        for d_ in range(2):
            cp("dve", Sin[:, d_, :, :], stctx[:, d_, :, :], (k_stctx,), (k_Sin,))
            order = range(8) if d_ == 0 else range(7, -1, -1)
            fo = 2 if d_ == 0 else 10
            for j in order:
                fj = flags[:, fo + j:fo + j + 1]
                stg, k_stg = stg_ring.next()
                dma("sp", stg[:], st_gath[j * 128:(j + 1) * 128, :], (k_st_dst,), (k_stg,))
                act(Ej[:], stg[:, 260 + 2 * d_:262 + 2 * d_], AF.Exp, (k_stg, k_const), (k_Ej,), scale=fj)
                ts("dve", Lj[:].rearrange("p r c -> p (r c)"), stg[:, d_ * 130:(d_ + 1) * 130], fj, None, ALU.mult, None,
                   (k_stg, k_const), (k_Ej,))
                for pr in range(2):
                    stt(Sin[:, d_, pr, :], Sin[:, d_, pr, :], Ej[:, pr:pr + 1], Lj[:, pr, :], ALU.mult, ALU.add, (k_Sin, k_Ej), (k_Sin,))


    wch_ring = Ring([(sb("wch%d" % i, [128, 8, 128], BF16), Trk("wch%d" % i)) for i in range(6)])
    xwin = sb("xwin", [128, 8, 770], BF16)
    k_xwin = Trk("xwin")
    ropew = sb("ropew", [128, 2, 768])
    k_ropew = Trk("ropew")
    sguW = sb("sguW", [128, 4, 128], BF16)
    sguB = sb("sguB", [128, 2, 128])
    poolW = sb("poolW", [128, 2, 64], BF16)
    k_lw = Trk("layerw")
    uT = sb("uT", [128, 2, 512], BF16)
    k_uT = Trk("uT")
    aqTz = sb("aqTz", [128, 2, 2, 512], BF16)
    k_aqT = Trk("aqT")
    memset("dve", aqTz[:], 0.0, (k_aqT,))
    qkz = sb("qkz", [128, 2, 2, 2, 128], BF16)
    k_qkz = Trk("qkz")
    poolWbd = sb("poolWbd", [128, 2, 128], BF16)
    memset("dve", poolWbd[:], 0.0, (k_lw,))
    akT = sb("akT", [128, 2, 768], BF16)
    k_akT = Trk("akT")
    akTc = sb("akTc", [128, 2, 256], BF16)
    avc = sb("avc", [128, 2, 128], BF16)
    k_ctxkv = Trk("ctxkv")
    zpool = sb("zpool", [128, 6, 256], BF16)
    k_zpool = Trk("zpool")
    av = sb("av", [128, 6, 128], BF16)
    k_av = Trk("av")
    ropetmp = [sb("ropetmp%d" % i, [128, 512]) for i in range(2)]
    k_ropetmp = [Trk("ropetmp%d" % i) for i in range(2)]
    mixT = sb("mixT", [128, 8, 512], BF16)
    k_mixT = Trk("mixT")
    sgt = ropetmp
    k_sgt = k_ropetmp
    sv = sb("sv", [128, 256])
    svq = sb("svq", [128, 256])
    vhat = sb("vhat", [128, 256], BF16)
    lnst = sb("lnst", [128, 16])
    k_sv, k_vhat, k_lnst = Trk("sv"), Trk("vhat"), Trk("lnst")
    sgtmp = sb("sgtmp", [128, 128])
    k_sgtmp = Trk("sgtmp")
    qT_ring = Ring([(sb("qT%d" % i, [128, 2, 128], BF16), Trk("qT%d" % i)) for i in range(2)])
    osig = sb("osig", [128, 256])
    k_osig = Trk("osig")
    Aft = sb("Aft", [128, 2, 4, 128], BF16)
    k_Aft = Trk("Aft")
    Sfull = sb("Sfull", [128, 2, 2, 65], BF16)
    k_Sfull = Trk("Sfull")
    rr = sb("rr", [128, 24])
    k_rr = Trk("rr")
    hsum = sb("hsum", [128, 256])
    hsq = svq
    k_hsum = Trk("hsum")
    bmix = sb("bmix", [128, 256], BF16)
    k_bmix = Trk("bmix")
    s_m = sb("s_m", [128, 640])
    k_sm = Trk("s_m")
    p_b = sb("p_b", [128, 640], BF16)
    k_pb = Trk("p_b")
    pT = sb("pT", [128, 5, 128], BF16)
    k_pT = Trk("pT")
    ast = sb("ast", [128, 8])
    k_ast = Trk("ast")
    o_n = sb("o_n", [128, 256], BF16)
    k_on = Trk("o_n")
    pdT = sb("pdT", [128, 2, 128], BF16)
    k_pdT = Trk("pdT")
    ytmp = ropetmp[0]
    k_ytmp = k_ropetmp[0]

    def load_layer_weights(l):
        dma("pool", sguW[:], sgu_wT_in[:, l, :, :], (), (k_lw,))
        dma("sp", sguB[:], sgu_brep_in[:, l, :, :], (), (k_lw,))
        dma("pool", poolW[:], pool_w2_in[:, l, :, :], (), (k_lw,))
        for pr in range(2):
            for hh in range(2):
                cp("pool", poolWbd[hh * 64:(hh + 1) * 64, pr, hh * 64:(hh + 1) * 64], poolW[hh * 64:(hh + 1) * 64, pr, :], (k_lw,), (k_lw,))

    def wchunk(scr, ktr, par, c):
        wt, kw_ = wch_ring.next()
        dma("sp", wt[:], scr[par][c], (ktr[par][c],), (kw_,))
        return wt, kw_

    def fm_proj(par, c, rhs_cols, ncols, bank):
        wt, kw_ = wchunk(wfm_s, k_wfm, par, c)
        for kc in range(8):
            mm(PS[bank][:, 0:ncols], wt[:, kc, :], xwin[:, kc, rhs_cols:rhs_cols + ncols], kc == 0, kc == 7, (kw_, k_xwin), (kPS[bank],))

    def rope_apply(c_plain, c_perm, par, rhs_cols, ncols, rcol, dst, kdst, qz=None):
        fm_proj(par, c_plain, rhs_cols, ncols, 0)
        tt("dve", ropetmp[0][:, 0:ncols], PS[0][:, 0:ncols], ropew[:, 0, rcol:rcol + ncols], ALU.mult, (kPS[0], k_ropew), (k_ropetmp[0],))
        fm_proj(par, c_perm, rhs_cols, ncols, 1)
        tt("dve", ropetmp[1][:, 0:ncols], PS[1][:, 0:ncols], ropew[:, 1, rcol:rcol + ncols], ALU.mult, (kPS[1], k_ropew), (k_ropetmp[1],))
        if qz is None:
            tt("pool", dst, ropetmp[0][:, 0:ncols], ropetmp[1][:, 0:ncols], ALU.add, (k_ropetmp[0], k_ropetmp[1]), (kdst,))
        else:
            for hh in range(2):
                tt("pool", aqTz[hh * 64:(hh + 1) * 64, qz, hh, 0:ncols], ropetmp[0][hh * 64:(hh + 1) * 64, 0:ncols],
                   ropetmp[1][hh * 64:(hh + 1) * 64, 0:ncols], ALU.add, (k_ropetmp[0], k_ropetmp[1]), (kdst,))

    def phase2_block(l, j):
        par = l % 2
        is_ctx = j == NBLK
        last = l == nlayers - 1
        w = 1 if is_ctx else 0
        if is_ctx:
            tiles = [NT, NT + 1]
            wc0 = C_CTX - 1
            nwt = 2
            own0 = 0
            rc0 = C_CTX - 2
            xtr = (k_xnT[NBLK],)
        else:
            tiles = [4 * j + i for i in range(4)]
            wc0 = C_OWN + j * 512 - 128 - 1
            nwt = 6
            own0 = 1
            rc0 = C_OWN + j * 512 - 128 - 1
            xtr = tuple({k_xnT[j], k_xnT[j - 1] if j > 0 else k_xnT[NBLK], k_xnT[j + 1] if j < NBLK - 1 else k_xnT[NBLK]})
        nown = len(tiles)
        TW = nown * 128
        WW = nwt * 128
        ocol = 1 + own0 * 128
        dma("sp", xwin[:, :, 0:WW + 2], xnT_d[:, :, wc0:wc0 + WW + 2], xtr, (k_xwin,))
        dma("sp", ropew[:, :, 0:WW], rope_in[:, :, rc0:rc0 + WW], (), (k_ropew,))
        kv_only = is_ctx and last and ctx_final

        kdst_T, kk_ = (akTc, k_ctxkv) if is_ctx else (akT, k_akT)
        for c0_ in range(0, WW, 512):
            n_ = min(512, WW - c0_)
            for g in range(2):
                rope_apply(10 + g, 12 + g, par, 1 + c0_, n_, c0_, kdst_T[:, g, c0_:c0_ + n_], kk_)
        wt8, kw8 = wchunk(wtm_s, k_wtm, par, 8)
        for p in range(nwt):
            for kc in range(8):
                mm(PS[2][:, 0:128], xwin[:, kc, 1 + p * 128:1 + (p + 1) * 128], wt8[:, kc, :], kc == 0, kc == 7, (kw8, k_xwin), (kPS[2],))
            if is_ctx:
                cp("act", avc[:, p, :], PS[2][:, 0:128], (kPS[2],), (k_ctxkv,))
            else:
                cp("act", av[:, p, :], PS[2][:, 0:128], (kPS[2],), (k_av,))
        if kv_only:
            return
        wp2 = [wchunk(wtm_s, k_wtm, par, 2), wchunk(wtm_s, k_wtm, par, 3)]
        for p in range(nwt):
            for f in range(2):
                for kc in range(8):
                    mm(PS[3][:, f * 128:(f + 1) * 128], xwin[:, kc, 1 + p * 128:1 + (p + 1) * 128], wp2[f][0][:, kc, :], kc == 0, kc == 7,
                       (wp2[f][1], k_xwin), (kPS[3],))
            cp("act", zpool[:, p, :], PS[3][:, 0:256], (kPS[3],), (k_zpool,))
        for f in range(2):
            fm_proj(par, f, ocol, TW, f)
            act(uT[:, f, 0:TW], PS[f][:, 0:TW], AF.Gelu_apprx_tanh, (kPS[f],), (k_uT,))
        for f in range(2):
            rope_apply(6 + f, 8 + f, par, ocol, TW, own0 * 128, None, k_aqT, qz=f)

        def wresident(slot, scr, ktr, c):
            wt_, kw__ = wres[slot]
            dma("sp", wt_[:], scr[par][c], (ktr[par][c],), (kw__,))
            return wt_, kw__
        wsv = [wresident(f, wtm_s, k_wtm, f) for f in range(2)]
        wmo = [wresident(2 + f, wtm_s, k_wtm, 4 + f) for f in range(4)]
        wqk = [wresident(6 + f, wfm_s, k_wfm, 2 + f) for f in range(4)]
        for i, ti in enumerate(tiles):
            p = own0 + i
            tc = 1 + p * 128
            mc = i * 128
            for f in range(2):
                for kc in range(8):
                    mm(PS[4][:, f * 128:(f + 1) * 128], xwin[:, kc, tc:tc + 128], wsv[f][0][:, kc, :], kc == 0, kc == 7, (wsv[f][1], k_xwin), (kPS[4],))
            act(sv[:], PS[4][:, 0:256], AF.Gelu_apprx_tanh, (kPS[4],), (k_sv,))
            tt("pool", svq[:], sv[:], sv[:], ALU.mult, (k_sv,), (k_sv,))
            P.add("dve", lambda e: e.reduce_sum(lnst[:, 0:4], sv[:].rearrange("p (h d) -> p h d", d=64), mybir.AxisListType.X), (k_sv,), (k_lnst,))
            P.add("dve", lambda e: e.reduce_sum(lnst[:, 4:8], svq[:].rearrange("p (h d) -> p h d", d=64), mybir.AxisListType.X), (k_sv,), (k_lnst,))
            ts("dve", lnst[:, 0:4], lnst[:, 0:4], 1.0 / 64, None, ALU.mult, None, (k_lnst,), (k_lnst,))
            tt("dve", lnst[:, 8:12], lnst[:, 0:4], lnst[:, 0:4], ALU.mult, (k_lnst,), (k_lnst,))
            stt(lnst[:, 4:8], lnst[:, 4:8], 1.0 / 64, lnst[:, 8:12], ALU.mult, ALU.subtract, (k_lnst,), (k_lnst,))
            ts("dve", lnst[:, 4:8], lnst[:, 4:8], EPS, None, ALU.add, None, (k_lnst,), (k_lnst,))
            act(lnst[:, 4:8], lnst[:, 4:8], AF.Sqrt, (k_lnst,), (k_lnst,))
            recip(lnst[:, 4:8], lnst[:, 4:8], (k_lnst,), (k_lnst,))
            for h in range(4):
                ts("dve", vhat[:, h * 64:(h + 1) * 64], sv[:, h * 64:(h + 1) * 64], lnst[:, h:h + 1], lnst[:, 4 + h:5 + h], ALU.subtract, ALU.mult,
                   (k_sv, k_lnst), (k_vhat,))
            for f in range(2):
                for hh in range(2):
                    h = 2 * f + hh
                    mm(PS[5][hh * 64:(hh + 1) * 64, f * 128:(f + 1) * 128], vhat[:, h * 64:(h + 1) * 64], sguW[:, h, :], True, True,
                       (k_vhat, k_lw), (kPS[5],))
            for f in range(2):
                tt("dve", sgtmp[:], PS[5][:, f * 128:(f + 1) * 128], sguB[:, f, :], ALU.add, (kPS[5], k_lw), (k_sgtmp,))
                tt("pool", mixT[:, f, mc:mc + 128], sgtmp[:], uT[:, f, mc:mc + 128], ALU.mult, (k_sgtmp, k_uT), (k_mixT,))

            qT, kqT = qT_ring.next()
            kT, kkT = kT_ring.next()
            for half, (dstT, kd) in enumerate(((qT, kqT), (kT, kkT))):
                for f in range(2):
                    wt, kw_ = wqk[half * 2 + f]
                    for kc in range(8):
                        mm(PS[6][:, f * 130:(f + 1) * 130], wt[:, kc, :], xwin[:, kc, tc - 1:tc + 129], kc == 0, kc == 7, (kw_, k_xwin), (kPS[6],))
                conv_silu(PS[6], kPS[6], l, (2 * half, 2 * half + 1), dstT[:], kd, 0)
            for qk_, (srcT, ksrc) in enumerate(((qT, kqT), (kT, kkT))):
                for f in range(2):
                    for hh in range(2):
                        ts("pool", qkz[:, qk_, f, hh, :], srcT[:, f, :], cst[:, 2 + hh:3 + hh], None, ALU.mult, None, (ksrc, k_const), (k_qkz,))
            for f in range(4):
                for kc in range(8):
                    mm(PS[4][:, f * 128:(f + 1) * 128], xwin[:, kc, tc:tc + 128], wmo[f][0][:, kc, :], kc == 0, kc == 7, (wmo[f][1], k_xwin), (kPS[4],))
            act(osig[:], PS[4][:, 0:256], AF.Sigmoid, (kPS[4],), (k_osig,))
            v1, kv1 = v1_ring.next()
            cp("act", v1[:, :, 0:64], PS[4][:, 256:512].rearrange("p (h d) -> p h d", d=64), (kPS[4],), (kv1,))
            for h in range(4):
                hh, f = h % 2, h // 2
                mm(PS[5][:, h * 128:(h + 1) * 128], qkz[:, 1, f, hh, :], qT[:, f, :], True, True, (k_qkz, kqT), (kPS[5],))
            for d_ in range(2):
                mk = mk_f if d_ == 0 else mk_b
                for h in range(4):
                    stt(Aft[:, d_, h, :], PS[5][:, h * 128:(h + 1) * 128], wst[:, ti, d_ * 4 + h:d_ * 4 + h + 1], mk[:], ALU.mult, ALU.mult,
                        (kPS[5], k_wst, k_const), (k_Aft,))
            for d_ in range(2):
                dma("sp", Sld[:, d_, :, :], Sloc_d[d_, ti].rearrange("p (r c) -> p r c", r=2), (k_Slocd[d_][ti],), (k_Sld,))
            for d_ in range(2):
                for pr in range(2):
                    if is_ctx:
                        cp("pool", Sfull[:, pr, d_, :], Sld[:, d_, pr, :], (k_Sld,), (k_Sfull,))
                    else:
                        stt(Sfull[:, pr, d_, :], Sin[:, d_, pr, :], Ecum[:, ti, pr, d_:d_ + 1], Sld[:, d_, pr, :], ALU.mult, ALU.add,
                            (k_Sin, k_cum, k_Sld), (k_Sfull,))
            for d_ in range(2):
                bank = 0 + d_
                for h in range(4):
                    hh, f = h % 2, h // 2
                    mm(PS[bank][:, h * 65:(h + 1) * 65], Aft[:, d_, h, :], v1[:, h, :], True, False, (k_Aft, kv1), (kPS[bank],))
                    mm(PS[bank][:, h * 65:(h + 1) * 65], qkz[:, 0, f, hh, :], Sfull[:, f, d_, :], False, True,
                       (k_qkz, k_Sfull), (kPS[bank],))
            act(rr[:, 0:8], bst[:, ti, :], AF.Exp, (k_bst,), (k_rr,), scale=-1.0)
            for d_ in range(2):
                dcol = PS[d_][:, 0:260].rearrange("p (h c) -> p h c", c=65)[:, :, 64]
                act(rr[:, 8 + d_ * 4:12 + d_ * 4], dcol, AF.Abs, (kPS[d_],), (k_rr,))
            tt("dve", rr[:, 16:24], rr[:, 8:16], rr[:, 0:8], ALU.max, (k_rr,), (k_rr,))
            recip(rr[:, 16:24], rr[:, 16:24], (k_rr,), (k_rr,))
            for h in range(4):
                ts("dve", hsum[:, h * 64:(h + 1) * 64], PS[0][:, h * 65:h * 65 + 64], rr[:, 16 + h:17 + h], None, ALU.mult, None, (kPS[0], k_rr), (k_hsum,))
                stt(hsum[:, h * 64:(h + 1) * 64], PS[1][:, h * 65:h * 65 + 64], rr[:, 20 + h:21 + h], hsum[:, h * 64:(h + 1) * 64], ALU.mult, ALU.add,
                    (kPS[1], k_rr, k_hsum), (k_hsum,))
            tt("pool", hsq[:], hsum[:], hsum[:], ALU.mult, (k_hsum,), (k_hsum,))
            P.add("dve", lambda e: e.reduce_sum(lnst[:, 0:4], hsum[:].rearrange("p (h d) -> p h d", d=64), mybir.AxisListType.X), (k_hsum,), (k_lnst,))
            P.add("dve", lambda e: e.reduce_sum(lnst[:, 4:8], hsq[:].rearrange("p (h d) -> p h d", d=64), mybir.AxisListType.X), (k_hsum,), (k_lnst,))
            ts("dve", lnst[:, 0:4], lnst[:, 0:4], 1.0 / 64, None, ALU.mult, None, (k_lnst,), (k_lnst,))
            tt("dve", lnst[:, 8:12], lnst[:, 0:4], lnst[:, 0:4], ALU.mult, (k_lnst,), (k_lnst,))
            stt(lnst[:, 4:8], lnst[:, 4:8], 1.0 / 64, lnst[:, 8:12], ALU.mult, ALU.subtract, (k_lnst,), (k_lnst,))
            ts("dve", lnst[:, 4:8], lnst[:, 4:8], EPS, None, ALU.add, None, (k_lnst,), (k_lnst,))
            act(lnst[:, 4:8], lnst[:, 4:8], AF.Sqrt, (k_lnst,), (k_lnst,))
            recip(lnst[:, 4:8], lnst[:, 4:8], (k_lnst,), (k_lnst,))
            for h in range(4):
                ts("dve", hsum[:, h * 64:(h + 1) * 64], hsum[:, h * 64:(h + 1) * 64], lnst[:, h:h + 1], lnst[:, 4 + h:5 + h], ALU.subtract, ALU.mult,
                   (k_hsum, k_lnst), (k_hsum,))
            tt("pool", hsum[:], hsum[:], mnorm_grep[:, l, :], ALU.mult, (k_hsum, k_const), (k_hsum,))
            tt("pool", bmix[:], hsum[:], osig[:], ALU.mult, (k_hsum, k_osig), (k_bmix,))
            for f in range(2):
                tp(PSB[:, f * 128:(f + 1) * 128], bmix[:, f * 128:(f + 1) * 128], ident_b[:], (k_bmix, k_const), (kPSB,))
            cp("act", mixT[:, 2:4, mc:mc + 128], PSB[:, 0:256].rearrange("p (f t) -> p f t", f=2), (kPSB,), (k_mixT,))

            var = 0 if ti == 0 else (2 if ti == NT - 1 else 1)
            for h in range(4):
                hh, f, g = h % 2, h // 2, h // 2
                qh = aqTz[:, f, hh, mc:mc + 128]
                if is_ctx:
                    nk = 256
                    mm(PS[5][:, 0:256], qh, akTc[:, g, 0:256], True, True, (k_aqT, k_ctxkv), (kPS[5],))
                    ts("dve", s_m[:, 0:256], PS[5][:, 0:256], 0.125, None, ALU.mult, None, (kPS[5],), (k_sm,))
                else:
                    nk = 640
                    kc0 = (p - 1) * 128
                    mm(PS[5][:, 0:384], qh, akT[:, g, kc0:kc0 + 384], True, True, (k_aqT, k_akT), (kPS[5],))
                    mm(PS[6][:, 0:256], qh, akTc[:, g, 0:256], True, True, (k_aqT, k_ctxkv), (kPS[6],))
                    stt(s_m[:, 0:384], PS[5][:, 0:384], 0.125, amask[:, var, :], ALU.mult, ALU.add, (kPS[5], k_const), (k_sm,))
                    ts("dve", s_m[:, 384:640], PS[6][:, 0:256], 0.125, None, ALU.mult, None, (kPS[6],), (k_sm,))
                P.add("dve", lambda e, nk=nk: e.reduce_max(ast[:, 0:1], s_m[:, 0:nk], mybir.AxisListType.X), (k_sm,), (k_ast,))
                tt("dve", ast[:, 0:1], ast[:, 0:1], sink_rep[:, l, h:h + 1], ALU.max, (k_ast, k_const), (k_ast,))
                ts("dve", ast[:, 1:2], ast[:, 0:1], -1.0, None, ALU.mult, None, (k_ast,), (k_ast,))
                act(p_b[:, 0:nk], s_m[:, 0:nk], AF.Exp, (k_sm, k_ast), (k_pb, k_ast), bias=ast[:, 1:2], accum_out=ast[:, 2:3])
                act(ast[:, 3:4], sink_rep[:, l, h:h + 1], AF.Exp, (k_ast, k_const), (k_ast,), bias=ast[:, 1:2])
                tt("dve", ast[:, 4:5], ast[:, 2:3], ast[:, 3:4], ALU.add, (k_ast,), (k_ast,))
                recip(ast[:, 5:6], ast[:, 4:5], (k_ast,), (k_ast,))
                nkb = nk // 128
                for kb in range(nkb):
                    tp(PSB[:, kb * 128:(kb + 1) * 128], p_b[:, kb * 128:(kb + 1) * 128], ident_b[:], (k_pb, k_const), (kPSB,))
                cp("act", pT[:, 0:nkb, :], PSB[:, 0:nk].rearrange("p (k t) -> p k t", t=128), (kPSB,), (k_pT,))
                for kb in range(nkb):
                    if is_ctx:
                        vblk = avc[:, kb, g * 64:(g + 1) * 64]
                        vtr = k_ctxkv
                    elif kb < 3:
                        vblk = av[:, p - 1 + kb, g * 64:(g + 1) * 64]
                        vtr = k_av
                    else:
                        vblk = avc[:, kb - 3, g * 64:(g + 1) * 64]
                        vtr = k_ctxkv
                    mm(PS[4][:, h * 64:(h + 1) * 64], pT[:, kb, :], vblk, kb == 0, kb == nkb - 1, (k_pT, vtr), (kPS[4],))
                ts("dve", o_n[:, h * 64:(h + 1) * 64], PS[4][:, h * 64:(h + 1) * 64], ast[:, 5:6], None, ALU.mult, None, (kPS[4], k_ast), (k_on,))
            for f in range(2):
                tp(PSB[:, f * 128:(f + 1) * 128], o_n[:, f * 128:(f + 1) * 128], ident_b[:], (k_on, k_const), (kPSB,))
            cp("act", mixT[:, 4:6, mc:mc + 128], PSB[:, 0:256].rearrange("p (f t) -> p f t", f=2), (kPSB,), (k_mixT,))

            pv = (3 if ti == NT else 4) if is_ctx else (0 if ti == 0 else (2 if ti == NT - 1 else 1))
            for g in range(4):
                hh, pr = g % 2, g // 2
                nbs = []
                if not (is_ctx and i == 0):
                    nbs.append((p - 1, 5))
                nbs.append((p, pv))
                if not (is_ctx and i == nown - 1):
                    nbs.append((p + 1, 6))
                for n_i, (pp, mv) in enumerate(nbs):
                    mm(PS[5][hh * 64:(hh + 1) * 64, pr * 128:(pr + 1) * 128], zpool[:, pp, g * 64:(g + 1) * 64], pool_M[:, mv, g, :],
                       n_i == 0, n_i == len(nbs) - 1, (k_zpool, k_const), (kPS[5],))
            tt("dve", pdT[:], PS[5][:, 0:256].rearrange("p (r t) -> p r t", r=2), pool_IC[:, pv, :, :], ALU.mult, (kPS[5], k_const), (k_pdT,))
            for pr in range(2):
                mm(PS[6][:, pr * 128:(pr + 1) * 128], poolWbd[:, pr, :], pdT[:, pr, :], True, True, (k_lw, k_pdT), (kPS[6],))
            for pr in range(2):
                ts("dve", mixT[:, 6 + pr, mc:mc + 128], PS[6][:, pr * 128:(pr + 1) * 128], pool_scaleT[:, l, pr:pr + 1], None, ALU.mult, None,
                   (kPS[6], k_const), (k_mixT,))

        if "dbg_mixT" in dbg_t and l == dbg_layer:
            if is_ctx:
                dma("sp", dbg_t["dbg_mixT"][:, :, NTOK:NTOK + 256], mixT[:, :, 0:256], (k_mixT,), (k_dbg,))
            else:
                dma("sp", dbg_t["dbg_mixT"][:, :, j * 512:(j + 1) * 512], mixT[:], (k_mixT,), (k_dbg,))
        if stop_after == "mix":
            return

        r0 = R_CTX if is_ctx else j * 512
        xn2, kxn2 = xnb_ring.next()
        for pair in range(nown // 2):
            for c in range(8):
                wt, kw_ = wchunk(wout_s, k_wout, par, c)
                for tl in range(2):
                    i = pair * 2 + tl
                    bank = 2 * tl + c // 4
                    for kc in range(8):
                        mm(PS[bank][:, (c % 4) * 128:(c % 4 + 1) * 128], mixT[:, kc, i * 128:(i + 1) * 128], wt[:, kc, :], kc == 0, kc == 7,
                           (k_mixT, kw_), (kPS[bank],))
            for tl in range(2):
                i = pair * 2 + tl
                xt, kxt = xt_ring.next()
                dma("sp", xt[:], xbuf[r0 + i * 128:r0 + (i + 1) * 128, :], (k_xbuf[j],), (kxt,))
                residual(PS[2 * tl], kPS[2 * tl], PS[2 * tl + 1], kPS[2 * tl + 1], xt, kxt, G1rep)
                dma("sp", xbuf[r0 + i * 128:r0 + (i + 1) * 128, :], xt[:], (kxt,), (k_xbuf[j],))
                norm_transpose(xt[:], kxt, lambda kc: A2[:, w, kc:kc + 1], lambda kc: modT[:, 24 + kc, w:w + 1], xn2, kxn2, i * 128)
        for ht in range(HT):
            wg_, kwg = wchunk(wfi_s, k_wfi, par, ht)
            wu_, kwu = wchunk(wfi_s, k_wfi, par, HT + ht)
            bg = 0 + (ht % 2) * 2
            bu = 1 + (ht % 2) * 2
            for kc in range(8):
                mm(PS[bg][:, 0:TW], wg_[:, kc, :], xn2[:, kc, 0:TW], kc == 0, kc == 7, (kwg, kxn2), (kPS[bg],))
            for kc in range(8):
                mm(PS[bu][:, 0:TW], wu_[:, kc, :], xn2[:, kc, 0:TW], kc == 0, kc == 7, (kwu, kxn2), (kPS[bu],))
            sg, ksg = sgt[ht % 2], k_sgt[ht % 2]
            act(sg[:, 0:TW], PS[bg][:, 0:TW], AF.Silu, (kPS[bg],), (ksg,))
            tt("dve", hT[:, ht, 0:TW], sg[:, 0:TW], PS[bu][:, 0:TW], ALU.mult, (ksg, kPS[bu]), (k_hT,))
        for pair in range(nown // 2):
            for hk in range(HT):
                wt, kw_ = wch_ring.next()
                dma("sp", wt[:].rearrange("p a b -> p (a b)"), wfo_s[par][hk], (k_wfo[par][hk],), (kw_,))
                wv_ = wt[:].rearrange("p a b -> p (a b)")
                for tl in range(2):
                    i = pair * 2 + tl
                    for half in range(2):
                        bank = 2 * tl + half
                        mm(PS[bank][:, 0:512], hT[:, hk, i * 128:(i + 1) * 128], wv_[:, half * 512:(half + 1) * 512], hk == 0, hk == HT - 1,
                           (k_hT, kw_), (kPS[bank],))
            for tl in range(2):
                i = pair * 2 + tl
                ti = tiles[i]
                xt, kxt = xt_ring.next()
                dma("sp", xt[:], xbuf[r0 + i * 128:r0 + (i + 1) * 128, :], (k_xbuf[j],), (kxt,))
                residual(PS[2 * tl], kPS[2 * tl], PS[2 * tl + 1], kPS[2 * tl + 1], xt, kxt, G3rep)
                if is_ctx:
                    dma("sp", xbuf[R_CTX + i * 128:R_CTX + (i + 1) * 128, :], xt[:], (kxt,), (k_xbuf[NBLK],))
                    if ctx_out is not None:
                        dma("sp", ctx_out[i * 128:(i + 1) * 128, :], xt[:], (kxt,), (k_out,))
                elif last:
                    dma("sp", out_d[ti * 128:(ti + 1) * 128, :], xt[:], (kxt,), (k_out,))
                else:
                    dma("sp", xbuf[ti * 128:(ti + 1) * 128, :], xt[:], (kxt,), (k_xbuf[j],))
                    if ti == 0:
                        dma("sp", halo_src[0:128, :], xt[:], (kxt,), (k_halo_src,))
                    if ti == NT - 1:
                        dma("sp", halo_src[128:256, :], xt[:], (kxt,), (k_halo_src,))

    def residual(psA, kA, psB, kB, xt, kxt, Grep_):
        act(junk[:, 0:512], psA[:, 0:512], AF.Square, (kA, k_junk), (k_junk, k_stat), accum_out=stat[:, 9:10])
        act(junk[:, 512:1024], psB[:, 0:512], AF.Square, (kB, k_junk), (k_junk, k_stat), accum_out=stat[:, 10:11])
        tt("dve", stat[:, 9:10], stat[:, 9:10], stat[:, 10:11], ALU.add, (k_stat,), (k_stat,))
        ts("dve", stat[:, 9:10], stat[:, 9:10], 1.0 / D, EPS, ALU.mult, ALU.add, (k_stat,), (k_stat,))
        act(stat[:, 9:10], stat[:, 9:10], AF.Sqrt, (k_stat,), (k_stat,))
        recip(stat[:, 9:10], stat[:, 9:10], (k_stat,), (k_stat,))
        for half, (pp_, kp_) in enumerate(((psA, kA), (psB, kB))):
            stt(ytmp[:], pp_[:, 0:512], stat[:, 9:10], Grep_[:, half * 512:(half + 1) * 512], ALU.mult, ALU.mult, (kp_, k_stat, k_grep), (k_ytmp,))
            tt("pool", xt[:, half * 512:(half + 1) * 512], xt[:, half * 512:(half + 1) * 512], ytmp[:], ALU.add, (kxt, k_ytmp), (kxt,))

    convert_weights(0)
    for l in range(nlayers):
        phase0(l)
        phase1a(l, NBLK)
        for j in range(NBLK):
            phase1a(l, j)
        if stop_after == "1a":
            break
        load_layer_weights(l)
        phase1b(l)
        if mode == "A":
            break
        if l + 1 < nlayers:
            convert_weights(l + 1)
        if stop_after == "1b":
            break
        if not (l == nlayers - 1 and ctx_final) and stop_after != "mix":
            set_greps(1)
        phase2_block(l, NBLK)
        if stop_after != "mix":
            set_greps(0)
        for j in range(NBLK):
            phase2_block(l, j)

    if "dbg_xnT" in dbg_t:
        dma("sp", dbg_t["dbg_xnT"].ap(), xnT_d.ap(), tuple(k_xnT), (k_dbg,))
    if "dbg_mod" in dbg_t:
        dma("sp", dbg_t["dbg_mod"].ap(), modT[:], (k_mod,), (k_dbg,))
    if "dbg_g1rep" in dbg_t:
        dma("sp", dbg_t["dbg_g1rep"].ap(), G1rep[:], (k_mod,), (k_dbg,))

    print("sbuf bytes remaining per partition:", nc.sbuf_bytes_remaining() if callable(nc.sbuf_bytes_remaining) else nc.sbuf_bytes_remaining)
    if MAXOPS:
        P.ops = P.ops[:MAXOPS]
    nops = P.emit(nc, es)
    es.close()
    return nc, nops


_PROGRAM_CACHE = {}
_WKEYS = ("w_mod", "b_mod", "norm_g", "w_in", "w_out", "sgu_w", "sgu_b", "mlstm_conv_w", "mlstm_gate_b", "mlstm_norm_g",
          "attn_sink", "pool_w", "pool_scale", "w_ffn_in", "w_ffn_out")
_A_SKIP = ("w_out", "w_ffn_in", "w_ffn_out")


def _get_program(NT, mode, is_last=None):
    key = (NT, mode, is_last)
    if key not in _PROGRAM_CACHE:
        if mode == "fused":
            _PROGRAM_CACHE[key] = build_program(NT, nlayers=DEPTH, LW=DEPTH)[0]
        else:
            _PROGRAM_CACHE[key] = build_program(NT, nlayers=1, LW=1, mode=mode, is_last=is_last)[0]
    return _PROGRAM_CACHE[key]


def _layer_maps(inputs, l, x_cur, ctx_cur, NT):
    li = {k: np.asarray(inputs[k])[l:l + 1] for k in _WKEYS}
    li.update(x=x_cur, ctx=ctx_cur, c=np.asarray(inputs["c"]), c_ctx=np.asarray(inputs["c_ctx"]))
    maps = _host_prep(li, NT)
    NTOK = NT * 128
    N = x_cur.shape[1]
    for r in range(8):
        b, q = divmod(r, 4)
        halo = np.zeros((256, D), np.float32)
        if q > 0:
            halo[0:128] = x_cur[b, q * NTOK - 128:q * NTOK]
        if q < 3:
            halo[128:256] = x_cur[b, (q + 1) * NTOK:(q + 1) * NTOK + 128]
        maps[r]["halo_in"] = halo
    return maps


def kernel_multi(runner=None, **inputs):
    run = runner or (lambda nc, maps: run_bass_kernel_spmd(nc, maps, core_ids=list(range(8))).results)
    x_cur = np.asarray(inputs["x"], np.float32)
    ctx_cur = np.asarray(inputs["ctx"], np.float32)
    B, N, _ = x_cur.shape
    NT = N // (4 * 128)
    NTOK = NT * 128
    for l in range(DEPTH):
        is_last = l == DEPTH - 1
        maps = _layer_maps(inputs, l, x_cur, ctx_cur, NT)
        mapsA = [{k: v for k, v in m.items() if k not in _A_SKIP} for m in maps]
        resA = run(_get_program(NT, "A"), mapsA)
        st_all = np.ascontiguousarray(np.concatenate([np.asarray(resA[r]["st_out"], np.float32) for r in range(8)], axis=0))
        for m in maps:
            m["st_all"] = st_all
        resB = run(_get_program(NT, "B", is_last), maps)
        x_new = np.empty_like(x_cur)
        for r in range(8):
            b, q = divmod(r, 4)
            x_new[b, q * NTOK:(q + 1) * NTOK] = np.asarray(resB[r]["out"], np.float32)
        if not is_last:
            ctx_cur = np.stack([np.asarray(resB[0]["ctx_out"], np.float32), np.asarray(resB[4]["ctx_out"], np.float32)], axis=0)
        x_cur = x_new
    return x_cur


def kernel_fused(**inputs):
    x = np.asarray(inputs["x"])
    B, N, _ = x.shape
    NT = N // (4 * 128)
    maps = _host_prep(inputs, NT)
    nc = _get_program(NT, "fused")
    res = run_bass_kernel_spmd(nc, maps, core_ids=list(range(8)))
    NTOK = NT * 128
    out = np.empty((B, N, D), np.float32)
    for r in range(8):
        b, q = divmod(r, 4)
        out[b, q * NTOK:(q + 1) * NTOK] = np.asarray(res.results[r]["out"], np.float32)
    return out


def kernel(**inputs):
    return kernel_fused(**inputs)
```
